# Optimizing a Trainium2 kernel written in Bass

```python
import jax, jax.numpy as jnp
from jax import lax
import numpy as np

D_MODEL = 1024
BATCH = 1
SEQ = 16384
DEPTH = 4

CHUNK = 64
RET_HEADS = 4
RET_QK_DIM = 256
RET_V_DIM = 512
RET_QK_W = RET_HEADS * RET_QK_DIM
RET_V_W = RET_HEADS * RET_V_DIM
ROPE_THETA = 10000.0
CONV_CH = D_MODEL
CONV_WIDTH = 31
MEM_LEN = 256
X_HEADS = 4
X_HEAD_DIM = D_MODEL // X_HEADS
FFN_DIM = 2816
FFN_CONV_WIDTH = 3
IN_SPLITS = (RET_QK_W, RET_QK_W, RET_V_W, RET_V_W, 2 * CONV_CH, 2 * D_MODEL)
IN_W = sum(IN_SPLITS)
RMS_EPS = 1e-6
LN_EPS = 1e-5

kernel_name = "hybrid_retention_conformer_stream_block"


def rms_norm(x, g):
    xf = x.astype(jnp.float32)
    y = xf * lax.rsqrt(jnp.mean(xf * xf, axis=-1, keepdims=True) + RMS_EPS)
    return (y * g.astype(jnp.float32)).astype(x.dtype)


def layer_norm(x, g, b):
    xf = x.astype(jnp.float32)
    mu = jnp.mean(xf, axis=-1, keepdims=True)
    var = jnp.mean(jnp.square(xf - mu), axis=-1, keepdims=True)
    y = (xf - mu) * lax.rsqrt(var + LN_EPS)
    return (y * g.astype(jnp.float32) + b.astype(jnp.float32)).astype(x.dtype)


def causal_dwconv(x, w, b):
    width = w.shape[0]
    y = lax.conv_general_dilated(
        x, w[:, None, :].astype(x.dtype), window_strides=(1,),
        padding=[(width - 1, 0)], dimension_numbers=("NWC", "WIO", "NWC"),
        feature_group_count=x.shape[-1])
    return y + b.astype(x.dtype)


def rotary(x, cos, sin):
    half = x.shape[-1] // 2
    x1, x2 = x[..., :half], x[..., half:]
    return jnp.concatenate([x1 * cos - x2 * sin, x2 * cos + x1 * sin], axis=-1)


def chunk_retention(q, k, v):
    bsz, seq, heads, dk = q.shape
    dv = v.shape[-1]
    n_chunks = seq // CHUNK
    dt = q.dtype
    qc = q.reshape(bsz, n_chunks, CHUNK, heads, dk)
    kc = k.reshape(bsz, n_chunks, CHUNK, heads, dk)
    vc = v.reshape(bsz, n_chunks, CHUNK, heads, dv)

    log_gamma = jnp.log(1.0 - jnp.power(2.0, -5.0 - jnp.arange(heads, dtype=jnp.float32)))
    idx = jnp.arange(CHUNK, dtype=jnp.float32)
    dist = jnp.abs(idx[:, None] - idx[None, :])
    d_inner = jnp.exp(log_gamma[:, None, None] * dist).astype(dt)
    decay_q = jnp.exp(log_gamma[None, :] * (idx[:, None] + 1.0)).astype(dt)
    decay_k = jnp.exp(log_gamma[None, :] * (CHUNK - 1.0 - idx[:, None])).astype(dt)
    decay_chunk = jnp.exp(log_gamma * CHUNK).astype(dt)

    scores = jnp.einsum("bnchd,bnkhd->bnhck", qc, kc) * d_inner
    o_inner = jnp.einsum("bnhck,bnkhv->bnchv", scores, vc)

    kc_dec = kc * decay_k[:, :, None]
    xs = (jnp.moveaxis(qc, 1, 0), jnp.moveaxis(kc_dec, 1, 0), jnp.moveaxis(vc, 1, 0))

    def step(state, inp):
        q_i, k_i, v_i = inp
        cross = jnp.einsum("bchk,bhkv->bchv", q_i, state)
        state = state * decay_chunk[None, :, None, None] + jnp.einsum("bchk,bchv->bhkv", k_i, v_i)
        return state, cross

    state0 = jnp.zeros((bsz, heads, dk, dv), dt)
    _, cross = lax.scan(step, state0, xs)
    o_cross = jnp.moveaxis(cross, 0, 1) * decay_q[:, :, None]
    return (o_inner + o_cross).reshape(bsz, seq, heads, dv)


def head_group_norm(o, g):
    of = o.astype(jnp.float32)
    mu = jnp.mean(of, axis=-1, keepdims=True)
    var = jnp.mean(jnp.square(of - mu), axis=-1, keepdims=True)
    y = (of - mu) * lax.rsqrt(var + LN_EPS)
    bsz, seq, heads, dv = o.shape
    return (y.reshape(bsz, seq, heads * dv) * g.astype(jnp.float32)).astype(o.dtype)


def mixer_sublayer(u, cos, sin, w_in, b_gate, ret_gn_g, w_ret_out, conv_dw_w, conv_dw_b,
                   conv_ln_g, conv_ln_b, w_conv_out, b_conv_out, w_mix_out):
    bsz, seq, _ = u.shape
    p = u @ w_in
    cuts = np.cumsum(IN_SPLITS)[:-1].tolist()
    q, k, v, g_ret, c_in, gates = jnp.split(p, cuts, axis=-1)

    q = rotary(q.reshape(bsz, seq, RET_HEADS, RET_QK_DIM), cos, sin) * (RET_QK_DIM ** -0.5)
    k = rotary(k.reshape(bsz, seq, RET_HEADS, RET_QK_DIM), cos, sin)
    v = v.reshape(bsz, seq, RET_HEADS, RET_V_DIM)
    o = head_group_norm(chunk_retention(q, k, v), ret_gn_g)
    y_a = (jax.nn.silu(g_ret) * o) @ w_ret_out

    a, b = jnp.split(c_in, 2, axis=-1)
    c = a * jax.nn.sigmoid(b)
    c = causal_dwconv(c, conv_dw_w, conv_dw_b)
    c = jax.nn.silu(layer_norm(c, conv_ln_g, conv_ln_b))
    y_b = c @ w_conv_out + b_conv_out

    g_a, g_b = jnp.split(jax.nn.sigmoid(gates + b_gate), 2, axis=-1)
    return (g_a * y_a + g_b * y_b) @ w_mix_out


def memory_cross_attention(h, mem_n, w_xq, w_xkv, w_xo):
    bsz, seq, _ = h.shape
    q = (h @ w_xq).reshape(bsz, seq, X_HEADS, X_HEAD_DIM)
    k, v = jnp.split(mem_n @ w_xkv, 2, axis=-1)
    k = k.reshape(bsz, MEM_LEN, X_HEADS, X_HEAD_DIM)
    v = v.reshape(bsz, MEM_LEN, X_HEADS, X_HEAD_DIM)
    s = jnp.einsum("bshd,bmhd->bhsm", q, k).astype(jnp.float32) * (X_HEAD_DIM ** -0.5)
    pr = jax.nn.softmax(s, axis=-1).astype(h.dtype)
    o = jnp.einsum("bhsm,bmhd->bshd", pr, v).reshape(bsz, seq, D_MODEL)
    return o @ w_xo


def conv_ffn(h, w_up, ffn_dw_w, ffn_dw_b, w_down):
    val, gate = jnp.split(h @ w_up, 2, axis=-1)
    gate = causal_dwconv(gate, ffn_dw_w, ffn_dw_b)
    return (jax.nn.silu(gate) * val) @ w_down


def setup_inputs(seed: int = 0) -> dict:
    key = jax.random.key(seed)
    ks = iter(jax.random.split(key, 32))
    f32 = jnp.float32
    L, D = DEPTH, D_MODEL

    def w(shape, fan_in):
        return jax.random.normal(next(ks), shape, f32) * (fan_in ** -0.5)

    def gain(shape):
        return 1.0 + 0.02 * jax.random.normal(next(ks), shape, f32)

    def bias(shape):
        return 0.02 * jax.random.normal(next(ks), shape, f32)

    x = jax.random.normal(next(ks), (BATCH, SEQ, D), f32)
    mem = jax.random.normal(next(ks), (BATCH, MEM_LEN, D), f32)
    positions = jnp.broadcast_to(jnp.arange(SEQ, dtype=jnp.int32)[None, :], (BATCH, SEQ))
    return {
        "x": x,
        "mem": mem,
        "positions": positions,
        "norm_mix_g": gain((L, D)),
        "w_in": w((L, D, IN_W), D),
        "b_gate": bias((L, 2 * D)),
        "ret_gn_g": gain((L, RET_V_W)),
        "w_ret_out": w((L, RET_V_W, D), RET_V_W),
        "conv_dw_w": w((L, CONV_WIDTH, CONV_CH), CONV_WIDTH),
        "conv_dw_b": bias((L, CONV_CH)),
        "conv_ln_g": gain((L, CONV_CH)),
        "conv_ln_b": bias((L, CONV_CH)),
        "w_conv_out": w((L, CONV_CH, D), CONV_CH),
        "b_conv_out": bias((L, D)),
        "w_mix_out": w((L, D, D), D),
        "norm_xattn_g": gain((L, D)),
        "norm_mem_g": gain((L, D)),
        "w_xq": w((L, D, D), D),
        "w_xkv": w((L, D, 2 * D), D),
        "w_xo": w((L, D, D), D),
        "norm_ffn_g": gain((L, D)),
        "w_up": w((L, D, 2 * FFN_DIM), D),
        "ffn_dw_w": w((L, FFN_CONV_WIDTH, FFN_DIM), FFN_CONV_WIDTH),
        "ffn_dw_b": bias((L, FFN_DIM)),
        "w_down": w((L, FFN_DIM, D), FFN_DIM),
        "norm_final_g": gain((D,)),
    }


def reference(x, mem, positions, norm_mix_g, w_in, b_gate, ret_gn_g, w_ret_out,
              conv_dw_w, conv_dw_b, conv_ln_g, conv_ln_b, w_conv_out, b_conv_out,
              w_mix_out, norm_xattn_g, norm_mem_g, w_xq, w_xkv, w_xo, norm_ffn_g,
              w_up, ffn_dw_w, ffn_dw_b, w_down, norm_final_g):
    inv_freq = 1.0 / (ROPE_THETA ** (jnp.arange(0, RET_QK_DIM, 2, dtype=jnp.float32) / RET_QK_DIM))
    ang = positions.astype(jnp.float32)[..., None] * inv_freq
    cos = jnp.cos(ang)[:, :, None, :].astype(x.dtype)
    sin = jnp.sin(ang)[:, :, None, :].astype(x.dtype)

    h = x
    for l in range(DEPTH):
        u = rms_norm(h, norm_mix_g[l])
        h = h + mixer_sublayer(u, cos, sin, w_in[l], b_gate[l], ret_gn_g[l], w_ret_out[l],
                               conv_dw_w[l], conv_dw_b[l], conv_ln_g[l], conv_ln_b[l],
                               w_conv_out[l], b_conv_out[l], w_mix_out[l])
        mem_n = rms_norm(mem, norm_mem_g[l])
        h = h + memory_cross_attention(rms_norm(h, norm_xattn_g[l]), mem_n,
                                       w_xq[l], w_xkv[l], w_xo[l])
        h = h + conv_ffn(rms_norm(h, norm_ffn_g[l]), w_up[l], ffn_dw_w[l], ffn_dw_b[l], w_down[l])
    return rms_norm(h, norm_final_g)
```

```python
import math
from contextlib import ExitStack

import numpy as np
import concourse.bass as bass
import concourse.mybir as mybir
from concourse.bass_utils import run_bass_kernel_spmd

F32 = mybir.dt.float32
BF16 = mybir.dt.bfloat16
I32 = mybir.dt.int32
AF = mybir.ActivationFunctionType
ALU = mybir.AluOpType

NCORES = 8
D = 1024
SEQ = 16384
DEPTH = 4
TOK = SEQ // NCORES
EXT = 128
NT_M = (TOK + EXT) // 128
NT_P = TOK // 128
HEADS = 4
DK = 256
DV = 512
FFN = 2816
NFB = FFN // 128
MEM = 256
CW = 31
RMS_EPS = 1e-6
LN_EPS = 1e-5
LG = [math.log(1.0 - 2.0 ** (-5.0 - h)) for h in range(HEADS)]

O_NMIX = 0
O_BGATE = 8
O_CDW = 24
O_CDB = O_CDW + 8 * CW
O_CLG = O_CDB + 8
O_CLB = O_CLG + 8
O_BCO = O_CLB + 8
O_NXA = O_BCO + 8
O_NMEM = O_NXA + 8
O_NFFN = O_NMEM + 8
O_FDW = O_NFFN + 8
O_FDB = O_FDW + NFB * 3
NPC = O_FDB + NFB


class Buf:
    __slots__ = ("name", "w", "r")

    def __init__(self, name):
        self.name = name
        self.w = None
        self.r = {}


class Chan:
    def __init__(self, k, name):
        self.sem = k.new_sem(name)
        self.count = 0
        self.key = ("chan", name)


class Eng:
    def __init__(self, k, name, e, is_pe=False, has_sem=True):
        self.k = k
        self.name = name
        self.e = e
        self.is_pe = is_pe
        self.key = ("eng", name)
        self.sem = k.new_sem("s_" + name) if has_sem else None
        self.count = 0
        self.seen = {}

    def _wait(self, deps):
        best = {}
        for d in deps:
            if d is None:
                continue
            sem, val, key = d
            if key == self.key and self.is_pe:
                continue
            if self.seen.get(key, 0) >= val:
                continue
            if key not in best or best[key][1] < val:
                best[key] = d
        for key, (sem, val, _) in best.items():
            self.e.wait_ge(sem, val)
            self.seen[key] = val

    @staticmethod
    def _deps(reads, writes):
        deps = []
        for b in reads:
            deps.append(b.w)
        for b in writes:
            deps.append(b.w)
            deps.extend(b.r.values())
        return deps

    @staticmethod
    def _commit(tok, reads, writes):
        for b in reads:
            b.r[tok[2]] = tok
        for b in writes:
            b.w = tok
            b.r = {}

    def op(self, fn, reads=(), writes=()):
        self._wait(self._deps(reads, writes))
        ins = fn()
        self.count += 1
        ins.then_inc(self.sem, 1)
        tok = (self.sem, self.count, self.key)
        self._commit(tok, reads, writes)
        return tok

    def dma(self, chan, pairs, reads=(), writes=()):
        deps = self._deps(reads, writes)
        if chan.count:
            deps.append((chan.sem, chan.count, chan.key))
        self._wait(deps)
        for out_ap, in_ap in pairs:
            self.e.dma_start(out=out_ap, in_=in_ap).then_inc(chan.sem, 16)
            chan.count += 16
        tok = (chan.sem, chan.count, chan.key)
        self._commit(tok, reads, writes)
        return tok


class Rot:
    def __init__(self, k, name, shape, dt, n, psum=False):
        alloc = k.ps if psum else k.sb
        self.items = [(alloc(f"{name}{i}", shape, dt), Buf(f"{name}{i}")) for i in range(n)]
        self.i = 0

    def next(self):
        it = self.items[self.i % len(self.items)]
        self.i += 1
        return it


class K:
    def __init__(self, nc, es):
        self.nc = nc
        self.es = es
        self.es0 = es
        self.nsem = 0
        self.PE = Eng(self, "pe", nc.tensor, is_pe=True)
        self.ACT = Eng(self, "act", nc.scalar)
        self.DVE = Eng(self, "dve", nc.vector)
        self.POOL = Eng(self, "pool", nc.gpsimd)
        self.SP = Eng(self, "sp", nc.sync, has_sem=False)
        self.chans = []

    def new_sem(self, name):
        self.nsem += 1
        return self.es0.enter_context(self.nc.semaphore(name))

    def sb(self, name, shape, dt):
        return self.es.enter_context(self.nc.sbuf_tensor("sb_" + name, shape, dt))

    def ps(self, name, shape, dt):
        return self.es0.enter_context(self.nc.psum_tensor("ps_" + name, shape, dt))

    def chan(self, name):
        c = Chan(self, name)
        self.chans.append(c)
        return c

    def barrier(self):
        engs = [self.PE, self.ACT, self.DVE, self.POOL, self.SP]
        toks = [(e.sem, e.count, e.key) for e in engs if e.sem is not None and e.count > 0]
        toks += [(c.sem, c.count, c.key) for c in self.chans if c.count]
        for e in engs:
            e._wait(toks)

    def scope(self):
        k = self

        class _Scope:
            def __enter__(self_):
                self_.prev = k.es
                self_.st = ExitStack()
                k.es = self_.st
                return self_

            def __exit__(self_, *a):
                if a[0] is None:
                    k.barrier()
                k.es = self_.prev
                self_.st.close()
                return False
        return _Scope()

    def finish(self):
        for c in self.chans:
            if c.count:
                self.SP.e.wait_ge(c.sem, c.count)


def tok_blocks(W):
    out = []
    s = 0
    while s < W:
        n = min(512, W - s)
        out.append((s, n))
        s += n
    return out


def build(mode, dbg=False):
    nc = bass.Bass("TRN2", target_bir_lowering=False)
    isM = mode == "M"
    NT = NT_M if isM else NT_P
    W = NT * 128
    TB = tok_blocks(W)

    def din(name, shape, dt=F32):
        return nc.dram_tensor(name, list(shape), dt, kind="ExternalInput").ap()

    def dout(name, shape, dt=F32):
        return nc.dram_tensor(name, list(shape), dt, kind="ExternalOutput").ap()

    hwin = din("hwin", [NT_M * 128, D])
    posw = din("posw", [128, NT_M * 128], I32)
    w_in = din("w_in", [D, 10240])
    pcol_d = din("pcol", [128, NPC])
    if isM:
        sloc = din("sloc", [NCORES * HEADS * 2 * 128, DV])
        expo = din("expo", [128, NCORES])
        mem_d = din("mem", [MEM, D])
        w_ret = din("w_ret_out", [HEADS * DV, D])
        w_cvo = din("w_conv_out", [D, D])
        w_mix = din("w_mix_out", [D, D])
        w_xq = din("w_xq", [D, D])
        w_xkv = din("w_xkv", [D, 2 * D])
        w_xo = din("w_xo", [D, D])
        w_up = din("w_up", [D, 2 * FFN])
        w_down = din("w_down", [FFN, D])
        prow_d = din("prow", [128, 3072])
        tmask_d = din("tmask", [128, 128])
        uTi = din("uTi", [D, NT_P * 128], BF16)
        csi = din("csi", [256, NT_P * 128])
        kTi = din("kTi", [HEADS * DK, NT_P * 128], BF16)
        vi = din("vi", [NT_P * 128, HEADS * DV], BF16)
        hout = dout("hout", [TOK, D])
        nout = dout("nout", [TOK, D])
        goT_d = nc.dram_tensor("goT_d", [HEADS * DV, W], BF16).ap()
        hcur = nc.dram_tensor("hcur", [W, D], F32).ap()
        if dbg:
            dbg_h1 = dout("dbg_h1", [W, D])
            dbg_h2 = dout("dbg_h2", [W, D])
    else:
        sout = dout("sout", [HEADS * 2 * 128, DV])
        uTo = dout("uTo", [D, NT_P * 128], BF16)
        cso = dout("cso", [256, NT_P * 128])
        kTo = dout("kTo", [HEADS * DK, NT_P * 128], BF16)
        vo = dout("vo", [NT_P * 128, HEADS * DV], BF16)

    with ExitStack() as es:
        k = K(nc, es)
        PE, ACT, DVE, POOL, SP = k.PE, k.ACT, k.DVE, k.POOL, k.SP

        psA = [(k.ps(f"psA{i}", [128, 512], F32), Buf(f"psA{i}")) for i in range(6)]
        psT = Rot(k, "psT", [128, 1024], BF16, 2, psum=True)
        psi = [0]

        def bank():
            it = psA[psi[0] % 6]
            psi[0] += 1
            return it

        NSLOT = 4
        ring = [(k.sb(f"wslot{i}", [128, 8, 512], BF16), Buf(f"wslot{i}"), k.chan(f"c_w{i}"))
                for i in range(NSLOT)]
        ring_i = [0]

        def wload(pieces):
            slot, sbuf_, ch = ring[ring_i[0] % NSLOT]
            ring_i[0] += 1
            pairs = []
            for src, kc, off in pieces:
                n = src.shape[1]
                pairs.append((slot[:, 0:kc, off:off + n],
                              src.rearrange("(kc p) n -> p kc n", p=128)))
            POOL.dma(ch, pairs, writes=[sbuf_])
            return slot, sbuf_

        big2 = k.sb("big2", [128, 8, 2 + W], BF16)
        big2_b = [Buf(f"big2_{t}") for t in range(NT)]
        big2_halo = Buf("big2_halo")

        ident = k.sb("ident", [128, 128], BF16)
        ident_b = Buf("ident")
        pcol = k.sb("pcol", [128, NPC], F32)
        pcol_b = Buf("pcol")
        c_misc = k.chan("c_misc")

        col4 = Rot(k, "col", [128, 8], F32, 6)
        NB_ = 2
        junk = Rot(k, "junk", [128, 1024], BF16, NB_)
        hs_r = Rot(k, "hs", [128, 1024], BF16, NB_)
        hload = [(k.sb(f"hld{i}", [128, 1024], F32), Buf(f"hld{i}"), k.chan(f"c_h{i}")) for i in range(NB_)]
        hload_i = [0]
        tmpf = Rot(k, "tmpf", [128, 512], F32, 4)

        SP.dma(c_misc, [(pcol[:], pcol_d[:, :])], writes=[pcol_b])
        iot_i = k.sb("iot_i", [128, 128], I32)
        iot_f = k.sb("iot_f", [128, 128], F32)
        iot_b = Buf("iot")
        POOL.op(lambda: nc.gpsimd.iota(iot_i[:], [[1, 128]], base=0, channel_multiplier=-1), writes=[iot_b])
        DVE.op(lambda: nc.vector.tensor_copy(out=iot_f[:], in_=iot_i[:]), reads=[iot_b], writes=[iot_b])
        DVE.op(lambda: nc.vector.tensor_scalar(out=ident[:], in0=iot_f[:], scalar1=0.0, scalar2=None,
                                               op0=ALU.is_equal), reads=[iot_b], writes=[ident_b])
        absd = k.sb("absd", [128, 128], F32)
        absd_b = Buf("absd")
        ACT.op(lambda: nc.scalar.activation(out=absd[:], in_=iot_f[:], func=AF.Abs),
               reads=[iot_b], writes=[absd_b])
        colj_i = k.sb("colj_i", [128, 128], I32)
        colj = k.sb("colj", [128, 128], F32)
        colj_b = Buf("colj")
        POOL.op(lambda: nc.gpsimd.iota(colj_i[:], [[1, 128]], base=0, channel_multiplier=0), writes=[colj_b])
        DVE.op(lambda: nc.vector.tensor_copy(out=colj[:], in_=colj_i[:]), reads=[colj_b], writes=[colj_b])
        pidx_i = k.sb("pidx_i", [128, 1], I32)
        pidx = k.sb("pidx", [128, 1], F32)
        pidx_b = Buf("pidx")
        POOL.op(lambda: nc.gpsimd.iota(pidx_i[:], [[0, 1]], base=0, channel_multiplier=1), writes=[pidx_b])
        DVE.op(lambda: nc.vector.tensor_copy(out=pidx[:], in_=pidx_i[:]), reads=[pidx_b], writes=[pidx_b])

        d2s = k.sb("d2s", [128, HEADS, 128], F32)
        dqbc = k.sb("dqbc", [128, HEADS, 128], BF16)
        dkcol = k.sb("dkcol", [128, HEADS], F32)
        dec_b = Buf("dec")
        lsc = math.log(1.0 / 16.0)
        biasc = k.sb("biasc", [128, 16], F32)
        biasc_b = Buf("biasc")
        bias_vals = [lsc, RMS_EPS, LN_EPS, 0.0] + [LG[h] + lsc for h in range(HEADS)] + \
                    [127.0 * LG[h] for h in range(HEADS)]
        for i, v in enumerate(bias_vals):
            DVE.op(lambda i=i, v=v: nc.vector.memset(biasc[:, i:i + 1], float(v)), writes=[biasc_b])
        B_LSC, B_RMS, B_LN, B_ZERO, B_DQ, B_DK = 0, 1, 2, 3, 4, 8
        for h in range(HEADS):
            ACT.op(lambda h=h: nc.scalar.activation(out=d2s[:, h, :], in_=absd[:], func=AF.Exp,
                                                    scale=LG[h], bias=biasc[:, B_LSC:B_LSC + 1]),
                   reads=[absd_b, biasc_b], writes=[dec_b])
            ACT.op(lambda h=h: nc.scalar.activation(out=dqbc[:, h, :], in_=colj[:], func=AF.Exp,
                                                    scale=LG[h], bias=biasc[:, B_DQ + h:B_DQ + h + 1]),
                   reads=[colj_b, biasc_b], writes=[dec_b])
            ACT.op(lambda h=h: nc.scalar.activation(out=dkcol[:, h:h + 1], in_=pidx[:], func=AF.Exp,
                                                    scale=-LG[h], bias=biasc[:, B_DK + h:B_DK + h + 1]),
                   reads=[pidx_b, biasc_b], writes=[dec_b])
        DVE.op(lambda: nc.vector.memset(d2s[64:128, :, 0:64], 0.0), writes=[dec_b])

        DVE.op(lambda: nc.vector.memset(big2[:, :, 0:2], 0.0), writes=[big2_halo])

        def pe_group(mms, reads, writes):
            def fn():
                ins = None
                for (o, l, r, st, sp) in mms:
                    ins = nc.tensor.matmul(o, lhsT=l, rhs=r, start=st, stop=sp)
                return ins
            return PE.op(fn, reads=reads, writes=writes)

        def pe_transposes(pairs, reads, writes):
            def fn():
                ins = None
                for (o, i_) in pairs:
                    ins = nc.tensor.transpose(o, i_, ident[:])
                return ins
            return PE.op(fn, reads=list(reads) + [ident_b], writes=writes)

        def load_h_dram(t):
            tile_, b, ch = hload[hload_i[0] % len(hload)]
            hload_i[0] += 1
            SP.dma(ch, [(tile_[:], hwin[t * 128:(t + 1) * 128, :])], writes=[b])
            return tile_[:], b

        def rms_norm_T(get_tile, ntiles, gcol_off, dst, dst_bufs, dst_col0=2):
            for t in range(ntiles):
                xap, xb = get_tile(t)
                jt, jb = junk.next()
                ct, cb = col4.next()
                ACT.op(lambda: nc.scalar.activation(out=jt[:], in_=xap, func=AF.Square, accum_out=ct[:, 0:1]),
                       reads=[xb], writes=[jb, cb])
                ACT.op(lambda: nc.scalar.activation(out=ct[:, 1:2], in_=ct[:, 0:1], func=AF.Sqrt,
                                                    scale=1.0 / D, bias=biasc[:, B_RMS:B_RMS + 1]),
                       reads=[cb, biasc_b], writes=[cb])
                DVE.op(lambda: nc.vector.reciprocal(out=ct[:, 2:3], in_=ct[:, 1:2]), reads=[cb], writes=[cb])
                ht, hb = hs_r.next()
                DVE.op(lambda: nc.vector.tensor_scalar(out=ht[:], in0=xap, scalar1=ct[:, 2:3], scalar2=None,
                                                       op0=ALU.mult), reads=[xb, cb], writes=[hb])
                pt, pb = psT.next()
                pe_transposes([(pt[:, b * 128:(b + 1) * 128], ht[:, b * 128:(b + 1) * 128]) for b in range(8)],
                              reads=[hb], writes=[pb])
                c0 = dst_col0 + t * 128
                DVE.op(lambda: nc.vector.tensor_tensor(
                    out=dst[:, :, c0:c0 + 128],
                    in0=pt[:].rearrange("p (b t) -> p b t", b=8),
                    in1=pcol[:, gcol_off:gcol_off + 8].unsqueeze(2).to_broadcast([128, 8, 128]),
                    op=ALU.mult), reads=[pb, pcol_b], writes=[dst_bufs[t]])

        def uT(kc, s, n):
            return big2[:, kc, 2 + s:2 + s + n]

        def tiles_of(s, n):
            return list(range(s // 128, (s + n + 127) // 128))

        sB = k.scope()
        sB.__enter__()
        cosT = k.sb("cosT", [128, W], F32)
        sinT = k.sb("sinT", [128, W], F32)
        cs_b = Buf("cossin")
        with k.scope():
            CLO = NT_P * 128 if isM else 0
            WC = W - CLO
            posi = k.sb("posi", [128, WC], I32)
            ang = k.sb("ang", [128, WC], F32)
            rr = k.sb("rr", [128, WC], F32)
            nfi = k.sb("nfi", [128, WC], I32)
            nff = k.sb("nff", [128, WC], F32)
            c_cs = k.chan("c_cs")
            if isM:
                SP.dma(c_cs, [(cosT[:, 0:CLO], csi[0:128, :]), (sinT[:, 0:CLO], csi[128:256, :])], writes=[cs_b])
            rot_b = Buf("rotwork")
            invf = k.sb("invf", [128, 1], F32)
            SP.dma(c_misc, [(posi[:], posw[:, CLO:W])], writes=[rot_b])
            ACT.op(lambda: nc.scalar.activation(out=invf[:], in_=pidx[:], func=AF.Exp,
                                                scale=-math.log(10000.0) / 128.0),
                   reads=[pidx_b], writes=[rot_b])
            DVE.op(lambda: nc.vector.tensor_copy(out=ang[:], in_=posi[:]), reads=[rot_b], writes=[rot_b])
            DVE.op(lambda: nc.vector.tensor_scalar(out=ang[:], in0=ang[:], scalar1=invf[:, 0:1], scalar2=None,
                                                   op0=ALU.mult), reads=[rot_b], writes=[rot_b])
            TWO_PI = 2.0 * math.pi
            C1 = 6.28125
            C2 = TWO_PI - C1
            DVE.op(lambda: nc.vector.tensor_scalar(out=nff[:], in0=ang[:], scalar1=1.0 / TWO_PI, scalar2=None,
                                                   op0=ALU.mult), reads=[rot_b], writes=[rot_b])
            DVE.op(lambda: nc.vector.tensor_copy(out=nfi[:], in_=nff[:]), reads=[rot_b], writes=[rot_b])
            DVE.op(lambda: nc.vector.tensor_copy(out=nff[:], in_=nfi[:]), reads=[rot_b], writes=[rot_b])
            DVE.op(lambda: nc.vector.scalar_tensor_tensor(out=rr[:], in0=nff[:], scalar=-C1, in1=ang[:],
                                                          op0=ALU.mult, op1=ALU.add), reads=[rot_b], writes=[rot_b])
            DVE.op(lambda: nc.vector.scalar_tensor_tensor(out=rr[:], in0=nff[:], scalar=-C2, in1=rr[:],
                                                          op0=ALU.mult, op1=ALU.add), reads=[rot_b], writes=[rot_b])

            def wrap_and_sin(dst, shift):
                if shift != 0.0:
                    DVE.op(lambda: nc.vector.tensor_scalar(out=ang[:], in0=rr[:], scalar1=shift, scalar2=None,
                                                           op0=ALU.add), reads=[rot_b], writes=[rot_b])
                    src = ang
                else:
                    src = rr
                DVE.op(lambda: nc.vector.tensor_scalar(out=nff[:], in0=src[:], scalar1=math.pi, scalar2=None,
                                                       op0=ALU.is_gt), reads=[rot_b], writes=[rot_b])
                DVE.op(lambda: nc.vector.scalar_tensor_tensor(out=ang[:], in0=nff[:], scalar=-TWO_PI, in1=src[:],
                                                              op0=ALU.mult, op1=ALU.add), reads=[rot_b], writes=[rot_b])
                DVE.op(lambda: nc.vector.tensor_scalar(out=nff[:], in0=ang[:], scalar1=-math.pi, scalar2=None,
                                                       op0=ALU.is_lt), reads=[rot_b], writes=[rot_b])
                DVE.op(lambda: nc.vector.scalar_tensor_tensor(out=ang[:], in0=nff[:], scalar=TWO_PI, in1=ang[:],
                                                              op0=ALU.mult, op1=ALU.add), reads=[rot_b], writes=[rot_b])
                DVE.op(lambda: nc.vector.tensor_scalar(out=ang[:], in0=ang[:], scalar1=-3.1415925, scalar2=3.1415925,
                                                       op0=ALU.max, op1=ALU.min), reads=[rot_b], writes=[rot_b])
                ACT.op(lambda: nc.scalar.activation(out=dst[:, CLO:W], in_=ang[:], func=AF.Sin),
                       reads=[rot_b], writes=[cs_b])

            wrap_and_sin(sinT, 0.0)
            wrap_and_sin(cosT, math.pi / 2.0)


        c_ut = k.chan("c_ut")
        if isM:
            WPc = NT_P * 128
            SP.dma(c_ut, [(big2[:, :, 2:2 + WPc], uTi[:, :].rearrange("(kc p) t -> p kc t", p=128))],
                   writes=big2_b[0:NT_P])
            rms_norm_T(lambda _i: load_h_dram(NT_P), 1, O_NMIX, big2, [big2_b[NT_P]], dst_col0=2 + WPc)
        else:
            rms_norm_T(load_h_dram, NT, O_NMIX, big2, big2_b)
            SP.dma(c_ut, [(uTo[:, :].rearrange("(kc p) t -> p kc t", p=128), big2[:, :, 2:2 + W])], reads=big2_b)
            SP.dma(c_cs, [(cso[0:128, :], cosT[:, :]), (cso[128:256, :], sinT[:, :])], reads=[cs_b])

        if isM:
            _qT = k.sb("qT", [128, 2, W], BF16)
            _qb = [Buf(f"q{t}") for t in range(NT)]
            qT_l = [_qT, _qT]
            q_bl = [_qb, _qb]
            _kT = k.sb("kT", [128, 2, W], BF16)
            _vt = k.sb("vtm", [128, NT, DV], BF16)
            kT_l = [_kT, _kT]
            vtm_l = [_vt, _vt]
            _kb = [Buf(f"k{t}") for t in range(NT)]
            _vb = [Buf(f"v{t}") for t in range(NT)]
            k_bl = [_kb, _kb]
            v_bl = [_vb, _vb]
        else:
            qT_l = q_bl = None
            kT_l = [k.sb(f"kT{i}", [128, 2, W], BF16) for i in range(2)]
            vtm_l = [k.sb(f"vtm{i}", [128, NT, DV], BF16) for i in range(2)]
            k_bl = [[Buf(f"k{i}_{t}") for t in range(NT)] for i in range(2)]
            v_bl = [[Buf(f"v{i}_{t}") for t in range(NT)] for i in range(2)]
        S = k.sb("S", [128, 2, DV], F32)
        S_bf = k.sb("S_bf", [128, 2, DV], BF16)
        S_b = [Buf("S0"), Buf("S1")]
        Sbf_b = [Buf("S_bf0"), Buf("S_bf1")]
        sT_r = Rot(k, "sT", [128, 128], BF16, 2)
        qd_r = Rot(k, "qd", [128, 2, 128], BF16, 2)
        kdec_r = Rot(k, "kdec", [128, 256], BF16, 4)

        class SubRot:
            def __init__(self, items):
                self.items = items
                self.i = 0

            def next(self):
                it = self.items[self.i % len(self.items)]
                self.i += 1
                return it
        _b0, _b1 = psT.items[0][0], psT.items[1][0]
        pk_r = SubRot([(_b0[:, i * 256:(i + 1) * 256], Buf(f"pk{i}")) for i in range(4)])
        pgt_r = SubRot([(_b1[:, i * 512:(i + 1) * 512], Buf(f"pgt{i}")) for i in range(2)])
        c_so = k.chan("c_so")
        c_kv = [k.chan("c_kv0"), k.chan("c_kv1")]
        if isM:
            prow_b = Buf("gng")
            expo_t = k.sb("expo", [128, NCORES], F32)
            coef = k.sb("coef", [128, HEADS, NCORES], F32)
            coef_b = Buf("coef")
            SP.dma(c_misc, [(expo_t[:], expo[:, :])], writes=[coef_b])
            for h in range(HEADS):
                ACT.op(lambda h=h: nc.scalar.activation(out=coef[:, h, :], in_=expo_t[:], func=AF.Exp,
                                                        scale=LG[h] * TOK),
                       reads=[coef_b], writes=[coef_b])
            sl_r = [(k.sb(f"sl{i}", [128, 2, DV], F32), Buf(f"sl{i}"), k.chan(f"c_sl{i}")) for i in range(2)]
            sg_r = Rot(k, "sg", [128, DV], BF16, 3)
            oc_r = Rot(k, "oc", [128, DV], BF16, 8)
            sgg_r = Rot(k, "sgg", [128, DV], BF16, 8)
            var_r = Rot(k, "var", [128, 12], F32, 3)
            go_r = Rot(k, "go", [128, DV], BF16, 2)
            st_r = Rot(k, "bnst", [128, 8], F32, 3)
            gng_bf = k.sb("gng_bf", [128, 2048], BF16)
            c_gng = k.chan("c_gng")
            POOL.dma(c_gng, [(gng_bf[:], prow_d[:, 0:2048])], writes=[prow_b])
            goT_st = [(k.sb(f"goTst{i}", [128, 4, 512], BF16), Buf(f"goTst{i}"), k.chan(f"c_go{i}"))
                      for i in range(2)]
            goT_db = [[Buf(f"goTd{h}_{i}") for i in range(len(TB))] for h in range(HEADS)]

        def rotary_evac(pa, pb_, dst, s, n, dst_bufs):
            (x1, x1b), (x2, x2b) = pa, pb_
            t1, t1b = tmpf.next()
            t2, t2b = tmpf.next()
            cs = cosT[:, s:s + n]
            sn = sinT[:, s:s + n]
            wr = [dst_bufs[t] for t in tiles_of(s, n)]
            DVE.op(lambda: nc.vector.tensor_tensor(out=t1[:, 0:n], in0=x1[:, 0:n], in1=cs, op=ALU.mult),
                   reads=[x1b, cs_b], writes=[t1b])
            DVE.op(lambda: nc.vector.tensor_tensor(out=t2[:, 0:n], in0=x2[:, 0:n], in1=sn, op=ALU.mult),
                   reads=[x2b, cs_b], writes=[t2b])
            DVE.op(lambda: nc.vector.tensor_tensor(out=dst[:, 0, s:s + n], in0=t1[:, 0:n], in1=t2[:, 0:n],
                                                   op=ALU.subtract), reads=[t1b, t2b], writes=wr)
            t3, t3b = tmpf.next()
            t4, t4b = tmpf.next()
            DVE.op(lambda: nc.vector.tensor_tensor(out=t3[:, 0:n], in0=x2[:, 0:n], in1=cs, op=ALU.mult),
                   reads=[x2b, cs_b], writes=[t3b])
            DVE.op(lambda: nc.vector.tensor_tensor(out=t4[:, 0:n], in0=x1[:, 0:n], in1=sn, op=ALU.mult),
                   reads=[x1b, cs_b], writes=[t4b])
            DVE.op(lambda: nc.vector.tensor_tensor(out=dst[:, 1, s:s + n], in0=t3[:, 0:n], in1=t4[:, 0:n],
                                                   op=ALU.add), reads=[t3b, t4b], writes=wr)

        safe_i = [0]

        def bank_safe():
            it = psA[(0, 1, 4, 5)[safe_i[0] % 4]]
            safe_i[0] += 1
            return it

        def proj_fm(slot, sb_, col0, s, n, alloc=None):
            pt, pb = (alloc or bank)()
            pe_group([(pt[:, 0:n], slot[:, kc, col0:col0 + 128], uT(kc, s, n), kc == 0, kc == 7)
                      for kc in range(8)],
                     reads=[sb_] + [big2_b[t] for t in tiles_of(s, n)], writes=[pb])
            return pt, pb

        def proj_tm(slot, sb_, col0, ncols, t, kcn=8, src=None):
            pt, pb = bank()
            pe_group([(pt[:, 0:ncols], uT(kc, t * 128, 128), slot[:, kc, col0:col0 + ncols], kc == 0, kc == kcn - 1)
                      for kc in range(kcn)],
                     reads=[sb_, big2_b[t]], writes=[pb])
            return pt, pb

        WP = NT_P * 128

        def make_early(hh):
            pieces = [(w_in[:, 1024 + hh * DK:1024 + (hh + 1) * DK], 8, 256)]
            if isM:
                pieces.insert(0, (w_in[:, hh * DK:(hh + 1) * DK], 8, 0))
            sA, sA_b = wload(pieces)
            chunks = []
            if isM:
                qd_, qb_ = qT_l[hh % 2], q_bl[hh % 2]
                for (s_, n_) in TB:
                    def ch(s_=s_, n_=n_):
                        p1 = proj_fm(sA, sA_b, 0, s_, n_)
                        p2 = proj_fm(sA, sA_b, 128, s_, n_)
                        rotary_evac(p1, p2, qd_, s_, n_, qb_)
                    chunks.append(ch)
                return (sA, sA_b, None, None), chunks
            sB, sB_b = wload([(w_in[:, 2048 + hh * DV:2048 + (hh + 1) * DV], 8, 0)])
            kd_, kb_ = kT_l[hh % 2], k_bl[hh % 2]
            vd_, vb_ = vtm_l[hh % 2], v_bl[hh % 2]
            for bi_, (s_, n_) in enumerate(TB):
                def chk(s_=s_, n_=n_):
                    p1 = proj_fm(sA, sA_b, 256, s_, n_)
                    p2 = proj_fm(sA, sA_b, 384, s_, n_)
                    rotary_evac(p1, p2, kd_, s_, n_, kb_)
                chunks.append(chk)
                for t_ in tiles_of(s_, n_):
                    def chv(t_=t_):
                        pt, pb = proj_tm(sB, sB_b, 0, DV, t_)
                        ACT.op(lambda: nc.scalar.copy(out=vd_[:, t_, :], in_=pt[:]), reads=[pb], writes=[vb_[t_]])
                    chunks.append(chv)
            return (sA, sA_b, sB, sB_b), chunks

        early = {0: make_early(0)}
        for ch in early[0][1]:
            ch()

        for h in range(HEADS):
            if h not in early:
                early[h] = make_early(h)
                for ch in early[h][1]:
                    ch()
            (slotA, slotA_b, slotB, slotB_b), _ = early[h]
            kT, vtm = kT_l[h % 2], vtm_l[h % 2]
            k_b, v_b = k_bl[h % 2], v_bl[h % 2]
            if isM:
                qT, q_b = qT_l[h % 2], q_bl[h % 2]
                slotB, slotB_b = wload([(w_in[:, 2048 + h * DV:2048 + (h + 1) * DV], 8, 0)])
                SP.dma(c_kv[0], [(kT[:, :, 0:WP], kTi[h * DK:(h + 1) * DK, :].rearrange("(dc p) t -> p dc t", p=128))],
                       writes=k_b[0:NT_P])
                SP.dma(c_kv[1], [(vtm[:, 0:NT_P, :], vi[:, h * DV:(h + 1) * DV].rearrange("(t p) v -> p t v", p=128))],
                       writes=v_b[0:NT_P])
                for (s, n) in TB:
                    if s < WP:
                        continue
                    p1 = proj_fm(slotA, slotA_b, 256, s, n)
                    p2 = proj_fm(slotA, slotA_b, 384, s, n)
                    rotary_evac(p1, p2, kT, s, n, k_b)
                for t in range(NT_P, NT):
                    pt, pb = proj_tm(slotB, slotB_b, 0, DV, t)
                    ACT.op(lambda t=t, pt=pt: nc.scalar.copy(out=vtm[:, t, :], in_=pt[:]), reads=[pb], writes=[v_b[t]])
            else:
                SP.dma(c_kv[0], [(kTo[h * DK:(h + 1) * DK, :].rearrange("(dc p) t -> p dc t", p=128), kT[:, :, 0:WP])],
                       reads=k_b[0:NT_P])
                SP.dma(c_kv[1], [(vo[:, h * DV:(h + 1) * DV].rearrange("(t p) v -> p t v", p=128), vtm[:, 0:NT_P, :])],
                       reads=v_b[0:NT_P])
            if isM:
                slotC, slotC_b = wload([(w_in[:, 4096 + h * DV:4096 + (h + 1) * DV], 8, 0)])
                DVE.op(lambda: nc.vector.memset(S[:], 0.0), writes=S_b)
                for c in range(NCORES - 1):
                    sl, slb, slc = sl_r[c % 2]
                    r0 = (c * HEADS + h) * 256
                    SP.dma(slc, [(sl[:], sloc[r0:r0 + 256, :].rearrange("(dc p) v -> p dc v", p=128))],
                           writes=[slb])
                    DVE.op(lambda sl=sl, c=c, h=h: nc.vector.scalar_tensor_tensor(
                        out=S[:], in0=sl[:], scalar=coef[:, h, c:c + 1], in1=S[:], op0=ALU.mult, op1=ALU.add),
                        reads=[slb, coef_b] + S_b, writes=S_b)
            else:
                DVE.op(lambda: nc.vector.memset(S[:], 0.0), writes=S_b)
            ACT.op(lambda: nc.scalar.copy(out=S_bf[:], in_=S[:]), reads=S_b, writes=Sbf_b)
            if h + 1 < HEADS and not isM:
                early[h + 1] = make_early(h + 1)
                nxt = early[h + 1][1]
            else:
                nxt = []

            dect = math.exp(128.0 * LG[h])
            s2_q = []
            s2a_q = []
            s3_q = []
            blk = {}

            def flush(upto_t, final=False):
                while s2_q and (final or s2_q[0][0] <= upto_t - 1):
                    s2_q.pop(0)[1]()
                while s3_q and (final or s3_q[0][0] <= upto_t - 2):
                    s3_q.pop(0)[1]()


            def ktrans(t):
                c0_ = t * 128
                pk, pkb = psT.next()
                pe_transposes([(pk[:, dc * 128:(dc + 1) * 128], kT[:, dc, c0_:c0_ + 128]) for dc in range(2)],
                              reads=[k_b[t]], writes=[pkb])
                kd, kdb = kdec_r.next()
                ACT.op(lambda: nc.scalar.activation(out=kd[:], in_=pk[:, 0:256], func=AF.Copy,
                                                    scale=dkcol[:, h:h + 1]),
                       reads=[pkb, dec_b], writes=[kdb])
                return kd, kdb

            def scores(t):
                c0_ = t * 128
                psc, pscb = psA[0]
                pe_group([(psc[:, 0:128], kT[:, dc, c0_:c0_ + 128], qT[:, dc, c0_:c0_ + 128], dc == 0, dc == 1)
                          for dc in range(2)], reads=[k_b[t], q_b[t]], writes=[pscb])
                sT, sTb = sT_r.next()
                DVE.op(lambda: nc.vector.tensor_tensor(out=sT[:], in0=psc[:, 0:128], in1=d2s[:, h, :],
                                                       op=ALU.mult), reads=[pscb, dec_b], writes=[sTb])
                qd, qdb = qd_r.next()
                DVE.op(lambda: nc.vector.tensor_tensor(
                    out=qd[:], in0=qT[:, :, c0_:c0_ + 128],
                    in1=dqbc[:, h, :].unsqueeze(1).to_broadcast([128, 2, 128]), op=ALU.mult),
                    reads=[q_b[t], dec_b], writes=[qdb])
                return sT, sTb, qd, qdb

            kd_next = ktrans(0)
            sc_next = scores(0) if isM else None
            for t in range(NT):
                tb_i = t // 4
                c0 = t * 128
                if s2a_q:
                    s2a_q.pop(0)()
                kd, kdb = kd_next
                if t + 1 < NT:
                    kd_next = ktrans(t + 1)
                if isM:
                    sT, sTb, qd, qdb = sc_next
                if isM:
                    pg, pgb = psA[1]
                    pe_group([(pg[:], uT(kc, c0, 128), slotC[:, kc, :], kc == 0, kc == 7) for kc in range(8)],
                             reads=[slotC_b, big2_b[t]], writes=[pgb])
                    po, pob = psA[2 + (t % 2)]
                    pe_group([(po[:], sT[:], vtm[:, t, :], True, False),
                              (po[:], qd[:, 0, :], S_bf[:, 0, :], False, False),
                              (po[:], qd[:, 1, :], S_bf[:, 1, :], False, True)],
                             reads=[sTb, v_b[t], qdb, Sbf_b[0], Sbf_b[1]], writes=[pob])
                if isM:
                    d0, d0b = psA[4]
                    d1, d1b = psA[5]
                else:
                    d0, d0b = bank()
                    d1, d1b = bank()
                pe_group([(d0[:], kd[:, 0:128], vtm[:, t, :], True, True),
                          (d1[:], kd[:, 128:256], vtm[:, t, :], True, True)],
                         reads=[kdb, v_b[t]], writes=[d0b, d1b])
                if isM:
                    sg, sgb = sg_r.next()
                    ACT.op(lambda: nc.scalar.activation(out=sg[:], in_=pg[:], func=AF.Silu),
                           reads=[pgb], writes=[sgb])
                for dc, (dd, ddb) in enumerate([(d0, d0b), (d1, d1b)]):
                    DVE.op(lambda: nc.vector.scalar_tensor_tensor(out=S[:, dc, :], in0=S[:, dc, :], scalar=dect,
                                                                  in1=dd[:], op0=ALU.mult, op1=ALU.add),
                           reads=[ddb, S_b[dc], Sbf_b[dc]], writes=[S_b[dc]])
                    if t < NT - 1:
                        ACT.op(lambda: nc.scalar.copy(out=S_bf[:, dc, :], in_=S[:, dc, :]),
                               reads=[S_b[dc]], writes=[Sbf_b[dc]])
                if isM and t + 1 < NT:
                    sc_next = scores(t + 1)
                for ci in range(t * len(nxt) // NT, (t + 1) * len(nxt) // NT):
                    nxt[ci]()
                flush(t)
                if not isM:
                    continue

                oc, ocb = oc_r.next()
                sgg, sggb = sgg_r.next()
                if t % 4 == 0:
                    vt, vtb = var_r.next()
                    blk[tb_i] = dict(vt=vt, vtb=vtb, tiles=[])
                bi_ = blk[tb_i]
                bi_["tiles"].append((t, oc, ocb, sgg, sggb))

                def stage2a(t=t, po=po, pob=pob, oc=oc, ocb=ocb, bi_=bi_):
                    i_ = t % 4
                    vt, vtb = bi_["vt"], bi_["vtb"]
                    stt, stb = st_r.next()
                    DVE.op(lambda: nc.vector.bn_stats(out=stt[:, 0:6], in_=po[:]), reads=[pob], writes=[stb])
                    DVE.op(lambda: nc.vector.bn_aggr(out=vt[:, 2 * i_:2 * i_ + 2], in_=stt[:, 0:6]),
                           reads=[stb], writes=[vtb])
                    ACT.op(lambda: nc.scalar.activation(out=oc[:], in_=po[:], func=AF.Identity, scale=-1.0,
                                                        bias=vt[:, 2 * i_:2 * i_ + 1]),
                           reads=[pob, vtb], writes=[ocb])
                s2a_q.append(stage2a)

                def stage2(sg=sg, sgb=sgb, sgg=sgg, sggb=sggb):
                    DVE.op(lambda: nc.vector.tensor_tensor(out=sgg[:], in0=sg[:], in1=gng_bf[:, h * DV:(h + 1) * DV],
                                                           op=ALU.mult), reads=[sgb, prow_b], writes=[sggb])
                s2_q.append((t, stage2))

                if t % 4 == 3 or t == NT - 1:
                    def stage3(tb_i=tb_i, bi_=bi_):
                        vt, vtb = bi_["vt"], bi_["vtb"]
                        nt_ = len(bi_["tiles"])
                        for i in range(nt_):
                            ACT.op(lambda i=i: nc.scalar.activation(out=vt[:, 8 + i:9 + i],
                                                                    in_=vt[:, 2 * i + 1:2 * i + 2], func=AF.Sqrt,
                                                                    bias=biasc[:, B_LN:B_LN + 1]),
                                   reads=[vtb, biasc_b], writes=[vtb])
                        DVE.op(lambda: nc.vector.reciprocal(out=vt[:, 8:8 + nt_], in_=vt[:, 8:8 + nt_]),
                               reads=[vtb], writes=[vtb])
                        DVE.op(lambda: nc.vector.tensor_scalar(out=vt[:, 8:8 + nt_], in0=vt[:, 8:8 + nt_],
                                                               scalar1=-1.0, scalar2=None, op0=ALU.mult),
                               reads=[vtb], writes=[vtb])
                        gst, gstb, gstc = goT_st[tb_i % 2]
                        for i, (t_, oc, ocb, sgg, sggb) in enumerate(bi_["tiles"]):
                            go, gob = go_r.next()
                            DVE.op(lambda: nc.vector.scalar_tensor_tensor(out=go[:], in0=oc[:],
                                                                          scalar=vt[:, 8 + i:9 + i], in1=sgg[:],
                                                                          op0=ALU.mult, op1=ALU.mult),
                                   reads=[ocb, sggb, vtb], writes=[gob])
                            pgt, pgtb = psT.next()
                            pe_transposes([(pgt[:, fc * 128:(fc + 1) * 128], go[:, fc * 128:(fc + 1) * 128])
                                           for fc in range(4)], reads=[gob], writes=[pgtb])
                            j0 = i * 128
                            ACT.op(lambda: nc.scalar.copy(out=gst[:, :, j0:j0 + 128],
                                                          in_=pgt[:, 0:512].rearrange("p (f t) -> p f t", f=4)),
                                   reads=[pgtb], writes=[gstb])
                        s_, n_ = TB[tb_i]
                        SP.dma(gstc, [(goT_d[h * DV:(h + 1) * DV, s_:s_ + n_].rearrange("(f p) t -> p f t", p=128),
                                       gst[:, :, 0:n_])], reads=[gstb], writes=[goT_db[h][tb_i]])
                    s3_q.append((t, stage3))
            while s2a_q:
                s2a_q.pop(0)()
            flush(NT, final=True)
            if not isM:
                SP.dma(c_so, [(sout[h * 256:(h + 1) * 256, :].rearrange("(dc p) v -> p dc v", p=128), S[:])],
                       reads=S_b)

        sB.__exit__(None, None, None)
        if not isM:
            k.finish()
            return nc

        hc_b = [Buf(f"hc{t}") for t in range(NT)]
        c_hst = [k.chan("c_hst0"), k.chan("c_hst1")]
        hres_r = Rot(k, "hres", [128, D], F32, 2)

        def load_hcur(t):
            tile_, b, ch = hload[hload_i[0] % len(hload)]
            hload_i[0] += 1
            SP.dma(ch, [(tile_[:], hcur[t * 128:(t + 1) * 128, :])], reads=[hc_b[t]], writes=[b])
            return tile_[:], b

        class HRing:
            def __init__(self, name, n):
                self.slots = [(k.sb(f"{name}{i}", [128, D], F32), Buf(f"{name}{i}"), k.chan(f"c_{name}{i}"))
                              for i in range(n)]
                self.i = 0
                self.loaded = {}

            def prefetch(self, t, src_ap, reads=()):
                tile_, b, ch = self.slots[self.i % len(self.slots)]
                self.i += 1
                SP.dma(ch, [(tile_[:], src_ap)], reads=list(reads), writes=[b])
                self.loaded[t] = (tile_, b)

            def update(self, t, psums):
                tile_, b = self.loaded.pop(t)
                for half, (pt, pb) in enumerate(psums):
                    DVE.op(lambda: nc.vector.tensor_tensor(out=tile_[:, half * 512:(half + 1) * 512], in0=pt[:],
                                                           in1=tile_[:, half * 512:(half + 1) * 512], op=ALU.add),
                           reads=[pb, b], writes=[b])
                SP.dma(c_hst[t % 2], [(hcur[t * 128:(t + 1) * 128, :], tile_[:])], reads=[b], writes=[hc_b[t]])

        HALO = 32
        sC = k.scope()
        sC.__enter__()
        ccT = k.sb("ccT", [128, 8, W], BF16)
        cc_b = [Buf(f"cc{i}") for i in range(len(TB))]
        with k.scope():
            ctc_r = Rot(k, "ctc", [128, HALO + W], BF16, 2)
            dg_r = Rot(k, "dg", [128, CW, 128], BF16, 2)
            sig_r = Rot(k, "sig", [128, 512], F32, 2)
            for cb2 in range(4):
                pieces = []
                for j in range(2):
                    cb = cb2 * 2 + j
                    pieces.append((w_in[:, 6144 + cb * 128:6144 + (cb + 1) * 128], 8, j * 256))
                    pieces.append((w_in[:, 7168 + cb * 128:7168 + (cb + 1) * 128], 8, j * 256 + 128))
                slot, slot_b = wload(pieces)
                for j in range(2):
                    cb = cb2 * 2 + j
                    ctc, ctcb = ctc_r.next()
                    DVE.op(lambda: nc.vector.memset(ctc[:, 0:HALO], 0.0), writes=[ctcb])
                    for (s, n) in TB:
                        pa, pab = proj_fm(slot, slot_b, j * 256, s, n)
                        pb2, pbb = proj_fm(slot, slot_b, j * 256 + 128, s, n)
                        sg, sgb = sig_r.next()
                        ACT.op(lambda: nc.scalar.activation(out=sg[:, 0:n], in_=pb2[:, 0:n], func=AF.Sigmoid),
                               reads=[pbb], writes=[sgb])
                        DVE.op(lambda: nc.vector.tensor_tensor(out=ctc[:, HALO + s:HALO + s + n], in0=pa[:, 0:n],
                                                               in1=sg[:, 0:n], op=ALU.mult),
                               reads=[pab, sgb], writes=[ctcb])
                    dg, dgb = dg_r.next()
                    DVE.op(lambda: nc.vector.tensor_tensor(
                        out=dg[:],
                        in0=ident[:].unsqueeze(1).to_broadcast([128, CW, 128]),
                        in1=pcol[:, O_CDW + cb * CW:O_CDW + (cb + 1) * CW].unsqueeze(2).to_broadcast([128, CW, 128]),
                        op=ALU.mult), reads=[ident_b, pcol_b], writes=[dgb])
                    for bi, (s, n) in enumerate(TB):
                        pc, pcb = bank()
                        o0 = HALO + s - (CW - 1)
                        pe_group([(pc[:, 0:n], dg[:, jj, :], ctc[:, o0 + jj:o0 + jj + n], jj == 0, jj == CW - 1)
                                  for jj in range(CW)], reads=[dgb, ctcb], writes=[pcb])
                        ACT.op(lambda: nc.scalar.activation(out=ccT[:, cb, s:s + n], in_=pc[:, 0:n],
                                                            func=AF.Identity,
                                                            bias=pcol[:, O_CDB + cb:O_CDB + cb + 1]),
                               reads=[pcb, pcol_b], writes=[cc_b[bi]])

        with k.scope():
            wcv0, wcv0_b = wload([(w_cvo[:, 0:512], 8, 0)])
            wcv1, wcv1_b = wload([(w_cvo[:, 512:1024], 8, 0)])
            wcv = [(wcv0, wcv0_b), (wcv1, wcv1_b)]
            ones_bf = k.sb("ones_bf", [128, 128], BF16)
            ones_b = Buf("ones")
            DVE.op(lambda: nc.vector.memset(ones_bf[:], 1.0), writes=[ones_b])
            sq_r = Rot(k, "sq", [128, 8, 512], BF16, 1)
            lnT_r = Rot(k, "lnT", [128, 8, 512], BF16, 1)
            mean_r = Rot(k, "mean", [128, 512], F32, 1)
            rstd_r = Rot(k, "rstdt", [128, 512], F32, 1)
            gl, glb, glc = k.sb("goTl", [128, 16, 512], BF16), Buf("goTl"), k.chan("c_gl")
            ga_r = Rot(k, "ga", [128, 512], F32, 2)
            gb_r = Rot(k, "gb", [128, 512], F32, 2)
            yb_r = Rot(k, "yb", [128, 512], F32, 2)
            wretl_r = [(k.sb(f"wretl{i}", [128, 16, 128], BF16), Buf(f"wretl{i}"), k.chan(f"c_wr{i}"))
                       for i in range(2)]
            wgl_r = [(k.sb(f"wgl{i}", [128, 8, 256], BF16), Buf(f"wgl{i}"), k.chan(f"c_wg{i}")) for i in range(2)]
            wl_i = 0

            for bi, (s, n) in enumerate(TB):
                sq, sqb = sq_r.next()
                DVE.op(lambda: nc.vector.tensor_tensor(out=sq[:, :, 0:n], in0=ccT[:, :, s:s + n],
                                                       in1=ccT[:, :, s:s + n], op=ALU.mult),
                       reads=[cc_b[bi]], writes=[sqb])
                p1, p1b = bank()
                p2, p2b = bank()
                pe_group([(p1[:, 0:n], ones_bf[:], ccT[:, cb, s:s + n], cb == 0, cb == 7) for cb in range(8)],
                         reads=[ones_b, cc_b[bi]], writes=[p1b])
                pe_group([(p2[:, 0:n], ones_bf[:], sq[:, cb, 0:n], cb == 0, cb == 7) for cb in range(8)],
                         reads=[ones_b, sqb], writes=[p2b])
                mean, meanb = mean_r.next()
                rstd, rstdb = rstd_r.next()
                ACT.op(lambda: nc.scalar.activation(out=mean[:, 0:n], in_=p1[:, 0:n], func=AF.Copy, scale=1.0 / D),
                       reads=[p1b], writes=[meanb])
                DVE.op(lambda: nc.vector.tensor_tensor(out=rstd[:, 0:n], in0=mean[:, 0:n], in1=mean[:, 0:n],
                                                       op=ALU.mult), reads=[meanb], writes=[rstdb])
                DVE.op(lambda: nc.vector.scalar_tensor_tensor(out=rstd[:, 0:n], in0=p2[:, 0:n], scalar=1.0 / D,
                                                              in1=rstd[:, 0:n], op0=ALU.mult, op1=ALU.subtract),
                       reads=[p2b, rstdb], writes=[rstdb])
                ACT.op(lambda: nc.scalar.activation(out=rstd[:, 0:n], in_=rstd[:, 0:n], func=AF.Sqrt,
                                                    bias=biasc[:, B_LN:B_LN + 1]),
                       reads=[rstdb, biasc_b], writes=[rstdb])
                DVE.op(lambda: nc.vector.reciprocal(out=rstd[:, 0:n], in_=rstd[:, 0:n]),
                       reads=[rstdb], writes=[rstdb])
                lnT, lnTb = lnT_r.next()
                for cb in range(8):
                    t1, t1b = tmpf.next()
                    DVE.op(lambda: nc.vector.tensor_tensor(out=t1[:, 0:n], in0=ccT[:, cb, s:s + n],
                                                           in1=mean[:, 0:n], op=ALU.subtract),
                           reads=[cc_b[bi], meanb], writes=[t1b])
                    DVE.op(lambda: nc.vector.tensor_tensor(out=t1[:, 0:n], in0=t1[:, 0:n], in1=rstd[:, 0:n],
                                                           op=ALU.mult), reads=[t1b, rstdb], writes=[t1b])
                    ACT.op(lambda: nc.scalar.activation(out=lnT[:, cb, 0:n], in_=t1[:, 0:n], func=AF.Silu,
                                                        scale=pcol[:, O_CLG + cb:O_CLG + cb + 1],
                                                        bias=pcol[:, O_CLB + cb:O_CLB + cb + 1]),
                           reads=[t1b, pcol_b], writes=[lnTb])
                SP.dma(glc, [(gl[:, :, 0:n], goT_d[:, s:s + n].rearrange("(f p) t -> p f t", p=128))],
                       reads=[goT_db[h][bi] for h in range(HEADS)], writes=[glb])
                for db in range(8):
                    wretl, wretl_b, c_wr = wretl_r[wl_i % 2]
                    wgl, wgl_b, c_wg = wgl_r[wl_i % 2]
                    wl_i += 1
                    POOL.dma(c_wg, [(wgl[:, :, 0:128],
                                     w_in[:, 8192 + db * 128:8192 + (db + 1) * 128]
                                     .rearrange("(kc p) n -> p kc n", p=128)),
                                    (wgl[:, :, 128:256],
                                     w_in[:, 9216 + db * 128:9216 + (db + 1) * 128]
                                     .rearrange("(kc p) n -> p kc n", p=128))],
                             writes=[wgl_b])
                    POOL.dma(c_wr, [(wretl[:], w_ret[:, db * 128:(db + 1) * 128]
                                     .rearrange("(kc p) n -> p kc n", p=128))], writes=[wretl_b])
                    wt, wtb = wcv[db // 4]
                    co = (db % 4) * 128
                    pyb, pybb = bank()
                    pe_group([(pyb[:, 0:n], wt[:, cb, co:co + 128], lnT[:, cb, 0:n], cb == 0, cb == 7)
                              for cb in range(8)], reads=[wtb, lnTb], writes=[pybb])
                    yb, ybb = yb_r.next()
                    ACT.op(lambda: nc.scalar.activation(out=yb[:, 0:n], in_=pyb[:, 0:n], func=AF.Identity,
                                                        bias=pcol[:, O_BCO + db:O_BCO + db + 1]),
                           reads=[pybb, pcol_b], writes=[ybb])
                    pga, pgab = proj_fm(wgl, wgl_b, 0, s, n)
                    pgb_, pgbb = proj_fm(wgl, wgl_b, 128, s, n)
                    ga, gab = ga_r.next()
                    gb, gbb = gb_r.next()
                    ACT.op(lambda: nc.scalar.activation(out=ga[:, 0:n], in_=pga[:, 0:n], func=AF.Sigmoid,
                                                        bias=pcol[:, O_BGATE + db:O_BGATE + db + 1]),
                           reads=[pgab, pcol_b], writes=[gab])
                    ACT.op(lambda: nc.scalar.activation(out=gb[:, 0:n], in_=pgb_[:, 0:n], func=AF.Sigmoid,
                                                        bias=pcol[:, O_BGATE + 8 + db:O_BGATE + 8 + db + 1]),
                           reads=[pgbb, pcol_b], writes=[gbb])
                    pya, pyab = bank()
                    pe_group([(pya[:, 0:n], wretl[:, fc, :], gl[:, fc, 0:n], fc == 0, fc == 15)
                              for fc in range(16)], reads=[wretl_b, glb], writes=[pyab])
                    DVE.op(lambda: nc.vector.tensor_tensor(out=ga[:, 0:n], in0=pya[:, 0:n], in1=ga[:, 0:n],
                                                           op=ALU.mult), reads=[pyab, gab], writes=[gab])
                    DVE.op(lambda: nc.vector.tensor_tensor(out=gb[:, 0:n], in0=yb[:, 0:n], in1=gb[:, 0:n],
                                                           op=ALU.mult), reads=[ybb, gbb], writes=[gbb])
                    DVE.op(lambda: nc.vector.tensor_tensor(out=ccT[:, db, s:s + n], in0=ga[:, 0:n], in1=gb[:, 0:n],
                                                           op=ALU.add), reads=[gab, gbb], writes=[cc_b[bi]])

        wm0, wm0_b = wload([(w_mix[:, 0:512], 8, 0)])
        wm1, wm1_b = wload([(w_mix[:, 512:1024], 8, 0)])
        ringC = HRing("hrc", 8)
        groups = [list(range(i, min(i + 4, NT))) for i in range(0, NT, 4)]
        for t in groups[0]:
            ringC.prefetch(t, hwin[t * 128:(t + 1) * 128, :])
        for gi, grp in enumerate(groups):
            if gi + 1 < len(groups):
                for t in groups[gi + 1]:
                    ringC.prefetch(t, hwin[t * 128:(t + 1) * 128, :])
            for t in grp:
                psums = []
                for (wt, wtb) in [(wm0, wm0_b), (wm1, wm1_b)]:
                    pt, pb = bank()
                    pe_group([(pt[:], ccT[:, kc, t * 128:(t + 1) * 128], wt[:, kc, :], kc == 0, kc == 7)
                              for kc in range(8)], reads=[cc_b[t // 4], wtb], writes=[pb])
                    psums.append((pt, pb))
                ringC.update(t, psums)
        sC.__exit__(None, None, None)
        if dbg:
            c_dbg = k.chan("c_dbg")
            for t in range(NT):
                SP.dma(c_dbg, [(dbg_h1[t * 128:(t + 1) * 128, :], hcur[t * 128:(t + 1) * 128, :])], reads=[hc_b[t]])

        with k.scope():
            memT = k.sb("memT", [128, 8, MEM], BF16)
            memT_b = [Buf("memT0"), Buf("memT1")]
            meml = [(k.sb(f"meml{i}", [128, D], F32), Buf(f"meml{i}")) for i in range(2)]
            c_mem = k.chan("c_mem")
            for i in range(2):
                SP.dma(c_mem, [(meml[i][0][:], mem_d[i * 128:(i + 1) * 128, :])], writes=[meml[i][1]])
            rms_norm_T(lambda t: (meml[t][0][:], meml[t][1]), 2, O_NMEM, memT, memT_b, dst_col0=0)
            kmT = k.sb("kmT", [128, 8, MEM], BF16)
            vm = k.sb("vm", [128, 2, D], BF16)
            km_b = Buf("kmT")
            vm_b = Buf("vm")
            for half in range(2):
                wk, wkb = wload([(w_xkv[:, half * 512:(half + 1) * 512], 8, 0)])
                for g in range(4):
                    pt, pb = bank()
                    pe_group([(pt[:, 0:MEM], wk[:, kc, g * 128:(g + 1) * 128], memT[:, kc, :], kc == 0, kc == 7)
                              for kc in range(8)], reads=[wkb] + memT_b, writes=[pb])
                    ACT.op(lambda: nc.scalar.copy(out=kmT[:, half * 4 + g, :], in_=pt[:, 0:MEM]),
                           reads=[pb], writes=[km_b])
            for half in range(2):
                wv, wvb = wload([(w_xkv[:, 1024 + half * 512:1024 + (half + 1) * 512], 8, 0)])
                for mc in range(2):
                    pt, pb = bank()
                    pe_group([(pt[:], memT[:, kc, mc * 128:(mc + 1) * 128], wv[:, kc, :], kc == 0, kc == 7)
                              for kc in range(8)], reads=[wvb] + memT_b, writes=[pb])
                    ACT.op(lambda: nc.scalar.copy(out=vm[:, mc, half * 512:(half + 1) * 512], in_=pt[:]),
                           reads=[pb], writes=[vm_b])

            rms_norm_T(load_hcur, NT, O_NXA, big2, big2_b)
            wq0, wq0_b = wload([(w_xq[:, 0:512], 8, 0)])
            wq1, wq1_b = wload([(w_xq[:, 512:1024], 8, 0)])
            wo0, wo0_b = wload([(w_xo[:, 0:512], 8, 0)])
            wo1, wo1_b = wload([(w_xo[:, 512:1024], 8, 0)])
            qx_r = Rot(k, "qx", [128, 8, 512], BF16, 1)
            ox_r = Rot(k, "ox", [128, 8, 512], BF16, 1)
            pexp_r = Rot(k, "pexp", [128, 4, MEM], F32, 2)
            pn_r = Rot(k, "pn", [128, 4, MEM], BF16, 2)
            pT_r = Rot(k, "pT", [128, 8, 128], BF16, 2)
            ringD = HRing("hrd", 8)
            for bi, (s, n) in enumerate(TB):
                for t in tiles_of(s, n):
                    ringD.prefetch(t, hcur[t * 128:(t + 1) * 128, :], reads=[hc_b[t]])
                qx, qxb = qx_r.next()
                for g in range(8):
                    wt, wtb = (wq0, wq0_b) if g < 4 else (wq1, wq1_b)
                    pt, pb = proj_fm(wt, wtb, (g % 4) * 128, s, n)
                    ACT.op(lambda: nc.scalar.copy(out=qx[:, g, 0:n], in_=pt[:, 0:n]), reads=[pb], writes=[qxb])
                ox, oxb = ox_r.next()

                def part_a(t):
                    j0 = t * 128 - s
                    sc = [bank(), bank()]
                    for hp in range(2):
                        mms = []
                        for hh in range(2):
                            hd = hp * 2 + hh
                            for dc in range(2):
                                mms.append((sc[hp][0][:, hh * MEM:(hh + 1) * MEM], qx[:, hd * 2 + dc, j0:j0 + 128],
                                            kmT[:, hd * 2 + dc, :], dc == 0, dc == 1))
                        pe_group(mms, reads=[qxb, km_b], writes=[sc[hp][1]])
                    ct, cb = col4.next()
                    for hp in range(2):
                        DVE.op(lambda hp=hp: nc.vector.tensor_reduce(
                            out=ct[:, hp * 2:hp * 2 + 2], in_=sc[hp][0][:].rearrange("p (h m) -> p h m", h=2),
                            axis=mybir.AxisListType.X, op=ALU.max), reads=[sc[hp][1]], writes=[cb])
                    DVE.op(lambda: nc.vector.tensor_scalar(out=ct[:, 0:4], in0=ct[:, 0:4], scalar1=-1.0 / 16.0,
                                                           scalar2=None, op0=ALU.mult), reads=[cb], writes=[cb])
                    pe_, peb = pexp_r.next()
                    for hd in range(4):
                        ACT.op(lambda hd=hd: nc.scalar.activation(
                            out=pe_[:, hd, :], in_=sc[hd // 2][0][:, (hd % 2) * MEM:(hd % 2 + 1) * MEM],
                            func=AF.Exp, scale=1.0 / 16.0, bias=ct[:, hd:hd + 1], accum_out=ct[:, 4 + hd:5 + hd]),
                            reads=[sc[hd // 2][1], cb], writes=[peb, cb])
                    DVE.op(lambda: nc.vector.reciprocal(out=ct[:, 4:8], in_=ct[:, 4:8]), reads=[cb], writes=[cb])
                    pn, pnb = pn_r.next()
                    DVE.op(lambda: nc.vector.tensor_tensor(out=pn[:], in0=pe_[:],
                                                           in1=ct[:, 4:8].unsqueeze(2).to_broadcast([128, 4, MEM]),
                                                           op=ALU.mult), reads=[peb, cb], writes=[pnb])
                    return pn, pnb

                def part_b(t, pn, pnb):
                    j0 = t * 128 - s
                    ptt, pttb = psT.next()
                    pe_transposes([(ptt[:, (hd * 2 + mc) * 128:(hd * 2 + mc + 1) * 128],
                                    pn[:, hd, mc * 128:(mc + 1) * 128]) for hd in range(4) for mc in range(2)],
                                  reads=[pnb], writes=[pttb])
                    pT, pTb = pT_r.next()
                    ACT.op(lambda: nc.scalar.copy(out=pT[:], in_=ptt[:].rearrange("p (g t) -> p g t", g=8)),
                           reads=[pttb], writes=[pTb])
                    po2 = [bank(), bank()]
                    for hp in range(2):
                        mms = []
                        for gg in range(4):
                            g = hp * 4 + gg
                            hd = g // 2
                            for mc in range(2):
                                mms.append((po2[hp][0][:, gg * 128:(gg + 1) * 128],
                                            vm[:, mc, g * 128:(g + 1) * 128],
                                            pT[:, hd * 2 + mc, :], mc == 0, mc == 1))
                        pe_group(mms, reads=[vm_b, pTb], writes=[po2[hp][1]])
                        ACT.op(lambda hp=hp: nc.scalar.copy(
                            out=ox[:, hp * 4:(hp + 1) * 4, j0:j0 + 128],
                            in_=po2[hp][0][:].rearrange("p (g t) -> p g t", g=4)),
                            reads=[po2[hp][1]], writes=[oxb])

                tl = tiles_of(s, n)
                prev = None
                for t in tl:
                    cur = (t,) + part_a(t)
                    if prev is not None:
                        part_b(*prev)
                    prev = cur
                part_b(*prev)
                for t in tiles_of(s, n):
                    j0 = t * 128 - s
                    psums = []
                    for (wt, wtb) in [(wo0, wo0_b), (wo1, wo1_b)]:
                        pt, pb = bank()
                        pe_group([(pt[:], ox[:, kc, j0:j0 + 128], wt[:, kc, :], kc == 0, kc == 7)
                                  for kc in range(8)], reads=[oxb, wtb], writes=[pb])
                        psums.append((pt, pb))
                    ringD.update(t, psums)

        if dbg:
            for t in range(NT):
                SP.dma(c_dbg, [(dbg_h2[t * 128:(t + 1) * 128, :], hcur[t * 128:(t + 1) * 128, :])], reads=[hc_b[t]])
        with k.scope():
            h_sb = k.sb("h_sb", [128, NT, D], F32)
            h_b = [Buf(f"h{t}") for t in range(NT)]
            c_hl = [k.chan("c_hl0"), k.chan("c_hl1")]
            for t in range(NT):
                SP.dma(c_hl[t % 2], [(h_sb[:, t, :], hcur[t * 128:(t + 1) * 128, :])],
                       reads=[hc_b[t]], writes=[h_b[t]])
            rms_norm_T(lambda t: (h_sb[:, t, :], h_b[t]), NT, O_NFFN, big2, big2_b)
            tmask = k.sb("tmask", [128, 128], F32)
            tmask_b = Buf("tmask")
            SP.dma(c_misc, [(tmask[:], tmask_d[:, :])], writes=[tmask_b])
            DVE.op(lambda: nc.vector.tensor_tensor(out=big2[:, :, 2:130], in0=big2[:, :, 2:130],
                                                   in1=tmask[:].unsqueeze(1).to_broadcast([128, 8, 128]),
                                                   op=ALU.mult), reads=[big2_b[0], tmask_b], writes=[big2_b[0]])
            FB = []
            s = 0
            while s < W:
                n = min(510, W - s)
                FB.append((s, n))
                s += n
            actT = k.sb("actT", [128, 4, W], BF16)
            act_b = Buf("actT")
            acc_r = tmpf
            sl_r2 = tmpf
            wd = k.sb("wd", [128, 4, D], BF16)
            wd_b = Buf("wd")
            c_wd = k.chan("c_wd")
            for g in range((NFB + 3) // 4):
                nfb = min(4, NFB - g * 4)
                wval, wval_b = wload([(w_up[:, g * 512:g * 512 + nfb * 128], 8, 0)])
                wgat, wgat_b = wload([(w_up[:, FFN + g * 512:FFN + g * 512 + nfb * 128], 8, 0)])
                POOL.dma(c_wd, [(wd[:, 0:nfb, :],
                                 w_down[g * 512:g * 512 + nfb * 128, :].rearrange("(kc p) n -> p kc n", p=128))],
                         writes=[wd_b])
                for fi in range(nfb):
                    fb = g * 4 + fi
                    for (s, n) in FB:
                        pgt, pgtb = bank()
                        pe_group([(pgt[:, 0:n + 2], wgat[:, kc, fi * 128:(fi + 1) * 128], big2[:, kc, s:s + n + 2],
                                   kc == 0, kc == 7) for kc in range(8)],
                                 reads=[wgat_b, big2_halo] + [big2_b[t] for t in tiles_of(max(s - 2, 0), n + 2)
                                                              if t < NT],
                                 writes=[pgtb])
                        pv, pvb = proj_fm(wval, wval_b, fi * 128, s, n)
                        acc, accb = acc_r.next()
                        w0 = pcol[:, O_FDW + fb * 3 + 0:O_FDW + fb * 3 + 1]
                        w1 = pcol[:, O_FDW + fb * 3 + 1:O_FDW + fb * 3 + 2]
                        w2 = pcol[:, O_FDW + fb * 3 + 2:O_FDW + fb * 3 + 3]
                        bb = pcol[:, O_FDB + fb:O_FDB + fb + 1]
                        DVE.op(lambda: nc.vector.tensor_scalar(out=acc[:, 0:n], in0=pgt[:, 2:n + 2], scalar1=w2,
                                                               scalar2=bb, op0=ALU.mult, op1=ALU.add),
                               reads=[pgtb, pcol_b], writes=[accb])
                        DVE.op(lambda: nc.vector.scalar_tensor_tensor(out=acc[:, 0:n], in0=pgt[:, 1:n + 1],
                                                                      scalar=w1, in1=acc[:, 0:n],
                                                                      op0=ALU.mult, op1=ALU.add),
                               reads=[pgtb, pcol_b, accb], writes=[accb])
                        DVE.op(lambda: nc.vector.scalar_tensor_tensor(out=acc[:, 0:n], in0=pgt[:, 0:n],
                                                                      scalar=w0, in1=acc[:, 0:n],
                                                                      op0=ALU.mult, op1=ALU.add),
                               reads=[pgtb, pcol_b, accb], writes=[accb])
                        sl, slb = sl_r2.next()
                        ACT.op(lambda: nc.scalar.activation(out=sl[:, 0:n], in_=acc[:, 0:n], func=AF.Silu),
                               reads=[accb], writes=[slb])
                        DVE.op(lambda: nc.vector.tensor_tensor(out=actT[:, fi, s:s + n], in0=pv[:, 0:n],
                                                               in1=sl[:, 0:n], op=ALU.mult),
                               reads=[pvb, slb], writes=[act_b])
                for t in range(NT):
                    for half in range(2):
                        pt, pb = bank()
                        pe_group([(pt[:], actT[:, fi, t * 128:(t + 1) * 128], wd[:, fi, half * 512:(half + 1) * 512],
                                   fi == 0, fi == nfb - 1) for fi in range(nfb)],
                                 reads=[act_b, wd_b], writes=[pb])
                        DVE.op(lambda: nc.vector.tensor_tensor(out=h_sb[:, t, half * 512:(half + 1) * 512],
                                                               in0=pt[:],
                                                               in1=h_sb[:, t, half * 512:(half + 1) * 512],
                                                               op=ALU.add),
                               reads=[pb, h_b[t]], writes=[h_b[t]])

            gfin = k.sb("gfin", [128, D], F32)
            gfin_b = Buf("gfin")
            SP.dma(c_misc, [(gfin[:], prow_d[:, 2048:3072])], writes=[gfin_b])
            c_out = [k.chan("c_out0"), k.chan("c_out1")]
            no_r = hres_r
            for t in range(1, NT):
                r0 = (t - 1) * 128
                SP.dma(c_out[0], [(hout[r0:r0 + 128, :], h_sb[:, t, :])], reads=[h_b[t]])
                jt, jb = junk.next()
                ct, cb = col4.next()
                ACT.op(lambda: nc.scalar.activation(out=jt[:], in_=h_sb[:, t, :], func=AF.Square,
                                                    accum_out=ct[:, 0:1]), reads=[h_b[t]], writes=[jb, cb])
                ACT.op(lambda: nc.scalar.activation(out=ct[:, 1:2], in_=ct[:, 0:1], func=AF.Sqrt, scale=1.0 / D,
                                                    bias=biasc[:, B_RMS:B_RMS + 1]),
                       reads=[cb, biasc_b], writes=[cb])
                DVE.op(lambda: nc.vector.reciprocal(out=ct[:, 2:3], in_=ct[:, 1:2]), reads=[cb], writes=[cb])
                no, nob = no_r.next()
                DVE.op(lambda: nc.vector.scalar_tensor_tensor(out=no[:], in0=h_sb[:, t, :], scalar=ct[:, 2:3],
                                                              in1=gfin[:], op0=ALU.mult, op1=ALU.mult),
                       reads=[h_b[t], cb, gfin_b], writes=[nob])
                SP.dma(c_out[1], [(nout[r0:r0 + 128, :], no[:])], reads=[nob])
        k.finish()
    return nc


_PROGS = {}


def _prog(mode):
    if mode not in _PROGS:
        _PROGS[mode] = build(mode)
    return _PROGS[mode]


def _colpack(v):
    v = np.asarray(v, np.float32)
    return np.ascontiguousarray(v.reshape(-1, 128).T)


def _pcol(inp, l):
    cols = [
        _colpack(inp["norm_mix_g"][l]),
        _colpack(inp["b_gate"][l]),
        np.ascontiguousarray(np.asarray(inp["conv_dw_w"][l], np.float32).reshape(CW, 8, 128).transpose(2, 1, 0)
                             .reshape(128, 8 * CW)),
        _colpack(inp["conv_dw_b"][l]),
        _colpack(inp["conv_ln_g"][l]),
        _colpack(inp["conv_ln_b"][l]),
        _colpack(inp["b_conv_out"][l]),
        _colpack(inp["norm_xattn_g"][l]),
        _colpack(inp["norm_mem_g"][l]),
        _colpack(inp["norm_ffn_g"][l]),
        np.ascontiguousarray(np.asarray(inp["ffn_dw_w"][l], np.float32).reshape(3, NFB, 128).transpose(2, 1, 0)
                             .reshape(128, NFB * 3)),
        _colpack(inp["ffn_dw_b"][l]),
    ]
    out = np.ascontiguousarray(np.concatenate(cols, axis=1))
    assert out.shape == (128, NPC)
    return out


def kernel(**inp):
    x = np.asarray(inp["x"], np.float32)[0]
    mem = np.ascontiguousarray(np.asarray(inp["mem"], np.float32)[0])
    pos = np.asarray(inp["positions"], np.int32)[0]
    cores = list(range(NCORES))
    WM = NT_M * 128

    def windows(arr, fill=0):
        pad = np.full((EXT,) + arr.shape[1:], fill, arr.dtype)
        ext = np.concatenate([pad, arr], axis=0)
        return [np.ascontiguousarray(ext[c * TOK:c * TOK + WM]) for c in cores]

    posw = [np.ascontiguousarray(np.broadcast_to(w[None, :], (128, WM))) for w in windows(pos)]
    expo = []
    for c in cores:
        e = np.array([float(c - 1 - cp) if cp < c else 1.0e4 for cp in cores], np.float32)
        expo.append(np.ascontiguousarray(np.broadcast_to(e[None, :], (128, NCORES))))

    tmask = [np.full((128, 128), 0.0 if c == 0 else 1.0, np.float32) for c in cores]
    h = x
    nrm = None
    for l in range(DEPTH):
        hw = windows(h)
        pcol = _pcol(inp, l)
        w_in = np.ascontiguousarray(np.asarray(inp["w_in"][l], np.float32))
        resP = run_bass_kernel_spmd(
            _prog("P"),
            [{"hwin": hw[c], "posw": posw[c], "w_in": w_in, "pcol": pcol} for c in cores],
            core_ids=cores)
        sloc = np.ascontiguousarray(np.concatenate([resP.results[c]["sout"] for c in cores], axis=0))
        kTi = [resP.results[c]["kTo"] for c in cores]
        uTi = [resP.results[c]["uTo"] for c in cores]
        csi = [resP.results[c]["cso"] for c in cores]
        vi = [resP.results[c]["vo"] for c in cores]
        prow = np.ascontiguousarray(np.broadcast_to(
            np.concatenate([np.asarray(inp["ret_gn_g"][l], np.float32),
                            np.asarray(inp["norm_final_g"], np.float32)])[None, :], (128, 3072)))
        common = {
            "w_in": w_in, "pcol": pcol, "sloc": sloc, "mem": mem, "prow": prow,
            "w_ret_out": np.ascontiguousarray(np.asarray(inp["w_ret_out"][l], np.float32)),
            "w_conv_out": np.ascontiguousarray(np.asarray(inp["w_conv_out"][l], np.float32)),
            "w_mix_out": np.ascontiguousarray(np.asarray(inp["w_mix_out"][l], np.float32)),
            "w_xq": np.ascontiguousarray(np.asarray(inp["w_xq"][l], np.float32)),
            "w_xkv": np.ascontiguousarray(np.asarray(inp["w_xkv"][l], np.float32)),
            "w_xo": np.ascontiguousarray(np.asarray(inp["w_xo"][l], np.float32)),
            "w_up": np.ascontiguousarray(np.asarray(inp["w_up"][l], np.float32)),
            "w_down": np.ascontiguousarray(np.asarray(inp["w_down"][l], np.float32)),
        }
        resM = run_bass_kernel_spmd(
            _prog("M"),
            [dict(common, hwin=hw[c], posw=posw[c], expo=expo[c], tmask=tmask[c], kTi=kTi[c], vi=vi[c], uTi=uTi[c], csi=csi[c]) for c in cores],
            core_ids=cores)
        h = np.concatenate([resM.results[c]["hout"] for c in cores], axis=0)
        nrm = [resM.results[c]["nout"] for c in cores]
    out = np.concatenate(nrm, axis=0).astype(np.float32)
    return out[None]
```

```python
import math
from contextlib import ExitStack

import numpy as np
import concourse.bass as bass
import concourse.mybir as mybir
from concourse.bass_utils import run_bass_kernel_spmd

F32 = mybir.dt.float32
BF16 = mybir.dt.bfloat16
I32 = mybir.dt.int32
AF = mybir.ActivationFunctionType
ALU = mybir.AluOpType

NCORES = 8
D = 1024
SEQ = 16384
DEPTH = 4
TOK = SEQ // NCORES
EXT = 128
NT_M = (TOK + EXT) // 128
NT_P = TOK // 128
HEADS = 4
DK = 256
DV = 512
FFN = 2816
NFB = FFN // 128
MEM = 256
CW = 31
RMS_EPS = 1e-6
LN_EPS = 1e-5
LG = [math.log(1.0 - 2.0 ** (-5.0 - h)) for h in range(HEADS)]

O_NMIX = 0
O_BGATE = 8
O_CDW = 24
O_CDB = O_CDW + 8 * CW
O_CLG = O_CDB + 8
O_CLB = O_CLG + 8
O_BCO = O_CLB + 8
O_NXA = O_BCO + 8
O_NMEM = O_NXA + 8
O_NFFN = O_NMEM + 8
O_FDW = O_NFFN + 8
O_FDB = O_FDW + NFB * 3
NPC = O_FDB + NFB


class Buf:
    __slots__ = ("name", "w", "r")

    def __init__(self, name):
        self.name = name
        self.w = None
        self.r = {}


class Chan:
    def __init__(self, k, name):
        self.sem = k.new_sem(name)
        self.count = 0
        self.key = ("chan", name)


class Eng:
    def __init__(self, k, name, e, is_pe=False, has_sem=True):
        self.k = k
        self.name = name
        self.e = e
        self.is_pe = is_pe
        self.key = ("eng", name)
        self.sem = k.new_sem("s_" + name) if has_sem else None
        self.count = 0
        self.seen = {}

    def _wait(self, deps):
        best = {}
        for d in deps:
            if d is None:
                continue
            sem, val, key = d
            if key == self.key and self.is_pe:
                continue
            if self.seen.get(key, 0) >= val:
                continue
            if key not in best or best[key][1] < val:
                best[key] = d
        for key, (sem, val, _) in best.items():
            self.e.wait_ge(sem, val)
            self.seen[key] = val

    @staticmethod
    def _deps(reads, writes):
        deps = []
        for b in reads:
            deps.append(b.w)
        for b in writes:
            deps.append(b.w)
            deps.extend(b.r.values())
        return deps

    @staticmethod
    def _commit(tok, reads, writes):
        for b in reads:
            b.r[tok[2]] = tok
        for b in writes:
            b.w = tok
            b.r = {}

    def op(self, fn, reads=(), writes=()):
        self._wait(self._deps(reads, writes))
        ins = fn()
        self.count += 1
        ins.then_inc(self.sem, 1)
        tok = (self.sem, self.count, self.key)
        self._commit(tok, reads, writes)
        return tok

    def dma(self, chan, pairs, reads=(), writes=()):
        deps = self._deps(reads, writes)
        if chan.count:
            deps.append((chan.sem, chan.count, chan.key))
        self._wait(deps)
        for out_ap, in_ap in pairs:
            self.e.dma_start(out=out_ap, in_=in_ap).then_inc(chan.sem, 16)
            chan.count += 16
        tok = (chan.sem, chan.count, chan.key)
        self._commit(tok, reads, writes)
        return tok


class Rot:
    def __init__(self, k, name, shape, dt, n, psum=False):
        alloc = k.ps if psum else k.sb
        self.items = [(alloc(f"{name}{i}", shape, dt), Buf(f"{name}{i}")) for i in range(n)]
        self.i = 0

    def next(self):
        it = self.items[self.i % len(self.items)]
        self.i += 1
        return it


class K:
    def __init__(self, nc, es):
        self.nc = nc
        self.es = es
        self.es0 = es
        self.nsem = 0
        self.PE = Eng(self, "pe", nc.tensor, is_pe=True)
        self.ACT = Eng(self, "act", nc.scalar)
        self.DVE = Eng(self, "dve", nc.vector)
        self.POOL = Eng(self, "pool", nc.gpsimd)
        self.SP = Eng(self, "sp", nc.sync, has_sem=False)
        self.chans = []

    def new_sem(self, name):
        self.nsem += 1
        return self.es0.enter_context(self.nc.semaphore(name))

    def sb(self, name, shape, dt):
        return self.es.enter_context(self.nc.sbuf_tensor("sb_" + name, shape, dt))

    def ps(self, name, shape, dt):
        return self.es0.enter_context(self.nc.psum_tensor("ps_" + name, shape, dt))

    def chan(self, name):
        c = Chan(self, name)
        self.chans.append(c)
        return c

    def barrier(self):
        engs = [self.PE, self.ACT, self.DVE, self.POOL, self.SP]
        toks = [(e.sem, e.count, e.key) for e in engs if e.sem is not None and e.count > 0]
        toks += [(c.sem, c.count, c.key) for c in self.chans if c.count]
        for e in engs:
            e._wait(toks)

    def scope(self):
        k = self

        class _Scope:
            def __enter__(self_):
                self_.prev = k.es
                self_.st = ExitStack()
                k.es = self_.st
                return self_

            def __exit__(self_, *a):
                if a[0] is None:
                    k.barrier()
                k.es = self_.prev
                self_.st.close()
                return False
        return _Scope()

    def finish(self):
        for c in self.chans:
            if c.count:
                self.SP.e.wait_ge(c.sem, c.count)


def tok_blocks(W):
    out = []
    s = 0
    while s < W:
        n = min(512, W - s)
        out.append((s, n))
        s += n
    return out


def build(mode, dbg=False):
    nc = bass.Bass("TRN2", target_bir_lowering=False)
    isM = mode == "M"
    NT = NT_M if isM else NT_P
    W = NT * 128
    TB = tok_blocks(W)

    def din(name, shape, dt=F32):
        return nc.dram_tensor(name, list(shape), dt, kind="ExternalInput").ap()

    def dout(name, shape, dt=F32):
        return nc.dram_tensor(name, list(shape), dt, kind="ExternalOutput").ap()

    hwin = din("hwin", [NT_M * 128, D])
    posw = din("posw", [128, NT_M * 128], I32)
    w_in = din("w_in", [D, 10240])
    pcol_d = din("pcol", [128, NPC])
    if isM:
        sloc = din("sloc", [NCORES * HEADS * 2 * 128, DV])
        expo = din("expo", [128, NCORES])
        mem_d = din("mem", [MEM, D])
        w_ret = din("w_ret_out", [HEADS * DV, D])
        w_cvo = din("w_conv_out", [D, D])
        w_mix = din("w_mix_out", [D, D])
        w_xq = din("w_xq", [D, D])
        w_xkv = din("w_xkv", [D, 2 * D])
        w_xo = din("w_xo", [D, D])
        w_up = din("w_up", [D, 2 * FFN])
        w_down = din("w_down", [FFN, D])
        prow_d = din("prow", [128, 3072])
        tmask_d = din("tmask", [128, 128])
        uTi = din("uTi", [D, NT_P * 128], BF16)
        csi = din("csi", [256, NT_P * 128])
        kTi = din("kTi", [HEADS * DK, NT_P * 128], BF16)
        vi = din("vi", [NT_P * 128, HEADS * DV], BF16)
        hout = dout("hout", [TOK, D])
        nout = dout("nout", [TOK, D])
        goT_d = nc.dram_tensor("goT_d", [HEADS * DV, W], BF16).ap()
        hcur = nc.dram_tensor("hcur", [W, D], F32).ap()
        if dbg:
            dbg_h1 = dout("dbg_h1", [W, D])
            dbg_h2 = dout("dbg_h2", [W, D])
    else:
        sout = dout("sout", [HEADS * 2 * 128, DV])
        uTo = dout("uTo", [D, NT_P * 128], BF16)
        cso = dout("cso", [256, NT_P * 128])
        kTo = dout("kTo", [HEADS * DK, NT_P * 128], BF16)
        vo = dout("vo", [NT_P * 128, HEADS * DV], BF16)

    with ExitStack() as es:
        k = K(nc, es)
        PE, ACT, DVE, POOL, SP = k.PE, k.ACT, k.DVE, k.POOL, k.SP

        psA = [(k.ps(f"psA{i}", [128, 512], F32), Buf(f"psA{i}")) for i in range(6)]
        psT = Rot(k, "psT", [128, 1024], BF16, 2, psum=True)
        psi = [0]

        def bank():
            it = psA[psi[0] % 6]
            psi[0] += 1
            return it

        NSLOT = 4
        ring = [(k.sb(f"wslot{i}", [128, 8, 512], BF16), Buf(f"wslot{i}"), k.chan(f"c_w{i}"))
                for i in range(NSLOT)]
        ring_i = [0]

        def wload(pieces):
            slot, sbuf_, ch = ring[ring_i[0] % NSLOT]
            ring_i[0] += 1
            pairs = []
            for src, kc, off in pieces:
                n = src.shape[1]
                pairs.append((slot[:, 0:kc, off:off + n],
                              src.rearrange("(kc p) n -> p kc n", p=128)))
            POOL.dma(ch, pairs, writes=[sbuf_])
            return slot, sbuf_

        big2 = k.sb("big2", [128, 8, 2 + W], BF16)
        big2_b = [Buf(f"big2_{t}") for t in range(NT)]
        big2_halo = Buf("big2_halo")

        ident = k.sb("ident", [128, 128], BF16)
        ident_b = Buf("ident")
        pcol = k.sb("pcol", [128, NPC], F32)
        pcol_b = Buf("pcol")
        c_misc = k.chan("c_misc")

        col4 = Rot(k, "col", [128, 8], F32, 6)
        NB_ = 2
        junk = Rot(k, "junk", [128, 1024], BF16, NB_)
        hs_r = Rot(k, "hs", [128, 1024], BF16, NB_)
        hload = [(k.sb(f"hld{i}", [128, 1024], F32), Buf(f"hld{i}"), k.chan(f"c_h{i}")) for i in range(NB_)]
        hload_i = [0]
        tmpf = Rot(k, "tmpf", [128, 512], F32, 4)

        SP.dma(c_misc, [(pcol[:], pcol_d[:, :])], writes=[pcol_b])
        iot_i = k.sb("iot_i", [128, 128], I32)
        iot_f = k.sb("iot_f", [128, 128], F32)
        iot_b = Buf("iot")
        POOL.op(lambda: nc.gpsimd.iota(iot_i[:], [[1, 128]], base=0, channel_multiplier=-1), writes=[iot_b])
        DVE.op(lambda: nc.vector.tensor_copy(out=iot_f[:], in_=iot_i[:]), reads=[iot_b], writes=[iot_b])
        DVE.op(lambda: nc.vector.tensor_scalar(out=ident[:], in0=iot_f[:], scalar1=0.0, scalar2=None,
                                               op0=ALU.is_equal), reads=[iot_b], writes=[ident_b])
        absd = k.sb("absd", [128, 128], F32)
        absd_b = Buf("absd")
        ACT.op(lambda: nc.scalar.activation(out=absd[:], in_=iot_f[:], func=AF.Abs),
               reads=[iot_b], writes=[absd_b])
        colj_i = k.sb("colj_i", [128, 128], I32)
        colj = k.sb("colj", [128, 128], F32)
        colj_b = Buf("colj")
        POOL.op(lambda: nc.gpsimd.iota(colj_i[:], [[1, 128]], base=0, channel_multiplier=0), writes=[colj_b])
        DVE.op(lambda: nc.vector.tensor_copy(out=colj[:], in_=colj_i[:]), reads=[colj_b], writes=[colj_b])
        pidx_i = k.sb("pidx_i", [128, 1], I32)
        pidx = k.sb("pidx", [128, 1], F32)
        pidx_b = Buf("pidx")
        POOL.op(lambda: nc.gpsimd.iota(pidx_i[:], [[0, 1]], base=0, channel_multiplier=1), writes=[pidx_b])
        DVE.op(lambda: nc.vector.tensor_copy(out=pidx[:], in_=pidx_i[:]), reads=[pidx_b], writes=[pidx_b])

        d2s = k.sb("d2s", [128, HEADS, 128], F32)
        dqbc = k.sb("dqbc", [128, HEADS, 128], BF16)
        dkcol = k.sb("dkcol", [128, HEADS], F32)
        dec_b = Buf("dec")
        lsc = math.log(1.0 / 16.0)
        biasc = k.sb("biasc", [128, 16], F32)
        biasc_b = Buf("biasc")
        bias_vals = [lsc, RMS_EPS, LN_EPS, 0.0] + [LG[h] + lsc for h in range(HEADS)] + \
                    [127.0 * LG[h] for h in range(HEADS)]
        for i, v in enumerate(bias_vals):
            DVE.op(lambda i=i, v=v: nc.vector.memset(biasc[:, i:i + 1], float(v)), writes=[biasc_b])
        B_LSC, B_RMS, B_LN, B_ZERO, B_DQ, B_DK = 0, 1, 2, 3, 4, 8
        for h in range(HEADS):
            ACT.op(lambda h=h: nc.scalar.activation(out=d2s[:, h, :], in_=absd[:], func=AF.Exp,
                                                    scale=LG[h], bias=biasc[:, B_LSC:B_LSC + 1]),
                   reads=[absd_b, biasc_b], writes=[dec_b])
            ACT.op(lambda h=h: nc.scalar.activation(out=dqbc[:, h, :], in_=colj[:], func=AF.Exp,
                                                    scale=LG[h], bias=biasc[:, B_DQ + h:B_DQ + h + 1]),
                   reads=[colj_b, biasc_b], writes=[dec_b])
            ACT.op(lambda h=h: nc.scalar.activation(out=dkcol[:, h:h + 1], in_=pidx[:], func=AF.Exp,
                                                    scale=-LG[h], bias=biasc[:, B_DK + h:B_DK + h + 1]),
                   reads=[pidx_b, biasc_b], writes=[dec_b])
        DVE.op(lambda: nc.vector.memset(d2s[64:128, :, 0:64], 0.0), writes=[dec_b])

        DVE.op(lambda: nc.vector.memset(big2[:, :, 0:2], 0.0), writes=[big2_halo])

        def pe_group(mms, reads, writes):
            def fn():
                ins = None
                for (o, l, r, st, sp) in mms:
                    ins = nc.tensor.matmul(o, lhsT=l, rhs=r, start=st, stop=sp)
                return ins
            return PE.op(fn, reads=reads, writes=writes)

        def pe_transposes(pairs, reads, writes):
            def fn():
                ins = None
                for (o, i_) in pairs:
                    ins = nc.tensor.transpose(o, i_, ident[:])
                return ins
            return PE.op(fn, reads=list(reads) + [ident_b], writes=writes)

        def load_h_dram(t):
            tile_, b, ch = hload[hload_i[0] % len(hload)]
            hload_i[0] += 1
            SP.dma(ch, [(tile_[:], hwin[t * 128:(t + 1) * 128, :])], writes=[b])
            return tile_[:], b

        def rms_norm_T(get_tile, ntiles, gcol_off, dst, dst_bufs, dst_col0=2):
            for t in range(ntiles):
                xap, xb = get_tile(t)
                jt, jb = junk.next()
                ct, cb = col4.next()
                ACT.op(lambda: nc.scalar.activation(out=jt[:], in_=xap, func=AF.Square, accum_out=ct[:, 0:1]),
                       reads=[xb], writes=[jb, cb])
                ACT.op(lambda: nc.scalar.activation(out=ct[:, 1:2], in_=ct[:, 0:1], func=AF.Sqrt,
                                                    scale=1.0 / D, bias=biasc[:, B_RMS:B_RMS + 1]),
                       reads=[cb, biasc_b], writes=[cb])
                DVE.op(lambda: nc.vector.reciprocal(out=ct[:, 2:3], in_=ct[:, 1:2]), reads=[cb], writes=[cb])
                ht, hb = hs_r.next()
                DVE.op(lambda: nc.vector.tensor_scalar(out=ht[:], in0=xap, scalar1=ct[:, 2:3], scalar2=None,
                                                       op0=ALU.mult), reads=[xb, cb], writes=[hb])
                pt, pb = psT.next()
                pe_transposes([(pt[:, b * 128:(b + 1) * 128], ht[:, b * 128:(b + 1) * 128]) for b in range(8)],
                              reads=[hb], writes=[pb])
                c0 = dst_col0 + t * 128
                DVE.op(lambda: nc.vector.tensor_tensor(
                    out=dst[:, :, c0:c0 + 128],
                    in0=pt[:].rearrange("p (b t) -> p b t", b=8),
                    in1=pcol[:, gcol_off:gcol_off + 8].unsqueeze(2).to_broadcast([128, 8, 128]),
                    op=ALU.mult), reads=[pb, pcol_b], writes=[dst_bufs[t]])

        def uT(kc, s, n):
            return big2[:, kc, 2 + s:2 + s + n]

        def tiles_of(s, n):
            return list(range(s // 128, (s + n + 127) // 128))

        sB = k.scope()
        sB.__enter__()
        cosT = k.sb("cosT", [128, W], F32)
        sinT = k.sb("sinT", [128, W], F32)
        cs_b = Buf("cossin")
        with k.scope():
            CLO = NT_P * 128 if isM else 0
            WC = W - CLO
            posi = k.sb("posi", [128, WC], I32)
            ang = k.sb("ang", [128, WC], F32)
            rr = k.sb("rr", [128, WC], F32)
            nfi = k.sb("nfi", [128, WC], I32)
            nff = k.sb("nff", [128, WC], F32)
            c_cs = k.chan("c_cs")
            if isM:
                SP.dma(c_cs, [(cosT[:, 0:CLO], csi[0:128, :]), (sinT[:, 0:CLO], csi[128:256, :])], writes=[cs_b])
            rot_b = Buf("rotwork")
            invf = k.sb("invf", [128, 1], F32)
            SP.dma(c_misc, [(posi[:], posw[:, CLO:W])], writes=[rot_b])
            ACT.op(lambda: nc.scalar.activation(out=invf[:], in_=pidx[:], func=AF.Exp,
                                                scale=-math.log(10000.0) / 128.0),
                   reads=[pidx_b], writes=[rot_b])
            DVE.op(lambda: nc.vector.tensor_copy(out=ang[:], in_=posi[:]), reads=[rot_b], writes=[rot_b])
            DVE.op(lambda: nc.vector.tensor_scalar(out=ang[:], in0=ang[:], scalar1=invf[:, 0:1], scalar2=None,
                                                   op0=ALU.mult), reads=[rot_b], writes=[rot_b])
            TWO_PI = 2.0 * math.pi
            C1 = 6.28125
            C2 = TWO_PI - C1
            DVE.op(lambda: nc.vector.tensor_scalar(out=nff[:], in0=ang[:], scalar1=1.0 / TWO_PI, scalar2=None,
                                                   op0=ALU.mult), reads=[rot_b], writes=[rot_b])
            DVE.op(lambda: nc.vector.tensor_copy(out=nfi[:], in_=nff[:]), reads=[rot_b], writes=[rot_b])
            DVE.op(lambda: nc.vector.tensor_copy(out=nff[:], in_=nfi[:]), reads=[rot_b], writes=[rot_b])
            DVE.op(lambda: nc.vector.scalar_tensor_tensor(out=rr[:], in0=nff[:], scalar=-C1, in1=ang[:],
                                                          op0=ALU.mult, op1=ALU.add), reads=[rot_b], writes=[rot_b])
            DVE.op(lambda: nc.vector.scalar_tensor_tensor(out=rr[:], in0=nff[:], scalar=-C2, in1=rr[:],
                                                          op0=ALU.mult, op1=ALU.add), reads=[rot_b], writes=[rot_b])

            def wrap_and_sin(dst, shift):
                if shift != 0.0:
                    DVE.op(lambda: nc.vector.tensor_scalar(out=ang[:], in0=rr[:], scalar1=shift, scalar2=None,
                                                           op0=ALU.add), reads=[rot_b], writes=[rot_b])
                    src = ang
                else:
                    src = rr
                DVE.op(lambda: nc.vector.tensor_scalar(out=nff[:], in0=src[:], scalar1=math.pi, scalar2=None,
                                                       op0=ALU.is_gt), reads=[rot_b], writes=[rot_b])
                DVE.op(lambda: nc.vector.scalar_tensor_tensor(out=ang[:], in0=nff[:], scalar=-TWO_PI, in1=src[:],
                                                              op0=ALU.mult, op1=ALU.add), reads=[rot_b], writes=[rot_b])
                DVE.op(lambda: nc.vector.tensor_scalar(out=nff[:], in0=ang[:], scalar1=-math.pi, scalar2=None,
                                                       op0=ALU.is_lt), reads=[rot_b], writes=[rot_b])
                DVE.op(lambda: nc.vector.scalar_tensor_tensor(out=ang[:], in0=nff[:], scalar=TWO_PI, in1=ang[:],
                                                              op0=ALU.mult, op1=ALU.add), reads=[rot_b], writes=[rot_b])
                DVE.op(lambda: nc.vector.tensor_scalar(out=ang[:], in0=ang[:], scalar1=-3.1415925, scalar2=3.1415925,
                                                       op0=ALU.max, op1=ALU.min), reads=[rot_b], writes=[rot_b])
                ACT.op(lambda: nc.scalar.activation(out=dst[:, CLO:W], in_=ang[:], func=AF.Sin),
                       reads=[rot_b], writes=[cs_b])

            wrap_and_sin(sinT, 0.0)
            wrap_and_sin(cosT, math.pi / 2.0)


        c_ut = k.chan("c_ut")
        if isM:
            WPc = NT_P * 128
            SP.dma(c_ut, [(big2[:, :, 2:2 + WPc], uTi[:, :].rearrange("(kc p) t -> p kc t", p=128))],
                   writes=big2_b[0:NT_P])
            rms_norm_T(lambda _i: load_h_dram(NT_P), 1, O_NMIX, big2, [big2_b[NT_P]], dst_col0=2 + WPc)
        else:
            rms_norm_T(load_h_dram, NT, O_NMIX, big2, big2_b)
            SP.dma(c_ut, [(uTo[:, :].rearrange("(kc p) t -> p kc t", p=128), big2[:, :, 2:2 + W])], reads=big2_b)
            SP.dma(c_cs, [(cso[0:128, :], cosT[:, :]), (cso[128:256, :], sinT[:, :])], reads=[cs_b])

        if isM:
            _qT = k.sb("qT", [128, 2, W], BF16)
            _qb = [Buf(f"q{t}") for t in range(NT)]
            qT_l = [_qT, _qT]
            q_bl = [_qb, _qb]
            _kT = k.sb("kT", [128, 2, W], BF16)
            _vt = k.sb("vtm", [128, NT, DV], BF16)
            kT_l = [_kT, _kT]
            vtm_l = [_vt, _vt]
            _kb = [Buf(f"k{t}") for t in range(NT)]
            _vb = [Buf(f"v{t}") for t in range(NT)]
            k_bl = [_kb, _kb]
            v_bl = [_vb, _vb]
        else:
            qT_l = q_bl = None
            kT_l = [k.sb(f"kT{i}", [128, 2, W], BF16) for i in range(2)]
            vtm_l = [k.sb(f"vtm{i}", [128, NT, DV], BF16) for i in range(2)]
            k_bl = [[Buf(f"k{i}_{t}") for t in range(NT)] for i in range(2)]
            v_bl = [[Buf(f"v{i}_{t}") for t in range(NT)] for i in range(2)]
        S = k.sb("S", [128, 2, DV], F32)
        S_bf = k.sb("S_bf", [128, 2, DV], BF16)
        S_b = [Buf("S0"), Buf("S1")]
        Sbf_b = [Buf("S_bf0"), Buf("S_bf1")]
        sT_r = Rot(k, "sT", [128, 128], BF16, 2)
        qd_r = Rot(k, "qd", [128, 2, 128], BF16, 2)
        kdec_r = Rot(k, "kdec", [128, 256], BF16, 4)

        class SubRot:
            def __init__(self, items):
                self.items = items
                self.i = 0

            def next(self):
                it = self.items[self.i % len(self.items)]
                self.i += 1
                return it
        _b0, _b1 = psT.items[0][0], psT.items[1][0]
        pk_r = SubRot([(_b0[:, i * 256:(i + 1) * 256], Buf(f"pk{i}")) for i in range(4)])
        pgt_r = SubRot([(_b1[:, i * 512:(i + 1) * 512], Buf(f"pgt{i}")) for i in range(2)])
        c_so = k.chan("c_so")
        c_kv = [k.chan("c_kv0"), k.chan("c_kv1")]
        if isM:
            prow_b = Buf("gng")
            expo_t = k.sb("expo", [128, NCORES], F32)
            coef = k.sb("coef", [128, HEADS, NCORES], F32)
            coef_b = Buf("coef")
            SP.dma(c_misc, [(expo_t[:], expo[:, :])], writes=[coef_b])
            for h in range(HEADS):
                ACT.op(lambda h=h: nc.scalar.activation(out=coef[:, h, :], in_=expo_t[:], func=AF.Exp,
                                                        scale=LG[h] * TOK),
                       reads=[coef_b], writes=[coef_b])
            sl_r = [(k.sb(f"sl{i}", [128, 2, DV], F32), Buf(f"sl{i}"), k.chan(f"c_sl{i}")) for i in range(2)]
            sg_r = Rot(k, "sg", [128, DV], BF16, 3)
            oc_r = Rot(k, "oc", [128, DV], BF16, 8)
            sgg_r = Rot(k, "sgg", [128, DV], BF16, 8)
            var_r = Rot(k, "var", [128, 12], F32, 3)
            go_r = Rot(k, "go", [128, DV], BF16, 2)
            st_r = Rot(k, "bnst", [128, 8], F32, 3)
            gng_bf = k.sb("gng_bf", [128, 2048], BF16)
            c_gng = k.chan("c_gng")
            POOL.dma(c_gng, [(gng_bf[:], prow_d[:, 0:2048])], writes=[prow_b])
            goT_st = [(k.sb(f"goTst{i}", [128, 4, 512], BF16), Buf(f"goTst{i}"), k.chan(f"c_go{i}"))
                      for i in range(2)]
            goT_db = [[Buf(f"goTd{h}_{i}") for i in range(len(TB))] for h in range(HEADS)]

        def rotary_evac(pa, pb_, dst, s, n, dst_bufs):
            (x1, x1b), (x2, x2b) = pa, pb_
            t1, t1b = tmpf.next()
            t2, t2b = tmpf.next()
            cs = cosT[:, s:s + n]
            sn = sinT[:, s:s + n]
            wr = [dst_bufs[t] for t in tiles_of(s, n)]
            DVE.op(lambda: nc.vector.tensor_tensor(out=t1[:, 0:n], in0=x1[:, 0:n], in1=cs, op=ALU.mult),
                   reads=[x1b, cs_b], writes=[t1b])
            DVE.op(lambda: nc.vector.tensor_tensor(out=t2[:, 0:n], in0=x2[:, 0:n], in1=sn, op=ALU.mult),
                   reads=[x2b, cs_b], writes=[t2b])
            DVE.op(lambda: nc.vector.tensor_tensor(out=dst[:, 0, s:s + n], in0=t1[:, 0:n], in1=t2[:, 0:n],
                                                   op=ALU.subtract), reads=[t1b, t2b], writes=wr)
            t3, t3b = tmpf.next()
            t4, t4b = tmpf.next()
            DVE.op(lambda: nc.vector.tensor_tensor(out=t3[:, 0:n], in0=x2[:, 0:n], in1=cs, op=ALU.mult),
                   reads=[x2b, cs_b], writes=[t3b])
            DVE.op(lambda: nc.vector.tensor_tensor(out=t4[:, 0:n], in0=x1[:, 0:n], in1=sn, op=ALU.mult),
                   reads=[x1b, cs_b], writes=[t4b])
            DVE.op(lambda: nc.vector.tensor_tensor(out=dst[:, 1, s:s + n], in0=t3[:, 0:n], in1=t4[:, 0:n],
                                                   op=ALU.add), reads=[t3b, t4b], writes=wr)

        safe_i = [0]

        def bank_safe():
            it = psA[(0, 1, 4, 5)[safe_i[0] % 4]]
            safe_i[0] += 1
            return it

        def proj_fm(slot, sb_, col0, s, n, alloc=None):
            pt, pb = (alloc or bank)()
            pe_group([(pt[:, 0:n], slot[:, kc, col0:col0 + 128], uT(kc, s, n), kc == 0, kc == 7)
                      for kc in range(8)],
                     reads=[sb_] + [big2_b[t] for t in tiles_of(s, n)], writes=[pb])
            return pt, pb

        def proj_tm(slot, sb_, col0, ncols, t, kcn=8, src=None):
            pt, pb = bank()
            pe_group([(pt[:, 0:ncols], uT(kc, t * 128, 128), slot[:, kc, col0:col0 + ncols], kc == 0, kc == kcn - 1)
                      for kc in range(kcn)],
                     reads=[sb_, big2_b[t]], writes=[pb])
            return pt, pb

        WP = NT_P * 128

        def make_early(hh):
            pieces = [(w_in[:, 1024 + hh * DK:1024 + (hh + 1) * DK], 8, 256)]
            if isM:
                pieces.insert(0, (w_in[:, hh * DK:(hh + 1) * DK], 8, 0))
            sA, sA_b = wload(pieces)
            chunks = []
            if isM:
                qd_, qb_ = qT_l[hh % 2], q_bl[hh % 2]
                for (s_, n_) in TB:
                    def ch(s_=s_, n_=n_):
                        p1 = proj_fm(sA, sA_b, 0, s_, n_)
                        p2 = proj_fm(sA, sA_b, 128, s_, n_)
                        rotary_evac(p1, p2, qd_, s_, n_, qb_)
                    chunks.append(ch)
                return (sA, sA_b, None, None), chunks
            sB, sB_b = wload([(w_in[:, 2048 + hh * DV:2048 + (hh + 1) * DV], 8, 0)])
            kd_, kb_ = kT_l[hh % 2], k_bl[hh % 2]
            vd_, vb_ = vtm_l[hh % 2], v_bl[hh % 2]
            for bi_, (s_, n_) in enumerate(TB):
                def chk(s_=s_, n_=n_):
                    p1 = proj_fm(sA, sA_b, 256, s_, n_)
                    p2 = proj_fm(sA, sA_b, 384, s_, n_)
                    rotary_evac(p1, p2, kd_, s_, n_, kb_)
                chunks.append(chk)
                for t_ in tiles_of(s_, n_):
                    def chv(t_=t_):
                        pt, pb = proj_tm(sB, sB_b, 0, DV, t_)
                        ACT.op(lambda: nc.scalar.copy(out=vd_[:, t_, :], in_=pt[:]), reads=[pb], writes=[vb_[t_]])
                    chunks.append(chv)
            return (sA, sA_b, sB, sB_b), chunks

        early = {0: make_early(0)}
        for ch in early[0][1]:
            ch()

        for h in range(HEADS):
            if h not in early:
                early[h] = make_early(h)
                for ch in early[h][1]:
                    ch()
            (slotA, slotA_b, slotB, slotB_b), _ = early[h]
            kT, vtm = kT_l[h % 2], vtm_l[h % 2]
            k_b, v_b = k_bl[h % 2], v_bl[h % 2]
            if isM:
                qT, q_b = qT_l[h % 2], q_bl[h % 2]
                slotB, slotB_b = wload([(w_in[:, 2048 + h * DV:2048 + (h + 1) * DV], 8, 0)])
                SP.dma(c_kv[0], [(kT[:, :, 0:WP], kTi[h * DK:(h + 1) * DK, :].rearrange("(dc p) t -> p dc t", p=128))],
                       writes=k_b[0:NT_P])
                SP.dma(c_kv[1], [(vtm[:, 0:NT_P, :], vi[:, h * DV:(h + 1) * DV].rearrange("(t p) v -> p t v", p=128))],
                       writes=v_b[0:NT_P])
                for (s, n) in TB:
                    if s < WP:
                        continue
                    p1 = proj_fm(slotA, slotA_b, 256, s, n)
                    p2 = proj_fm(slotA, slotA_b, 384, s, n)
                    rotary_evac(p1, p2, kT, s, n, k_b)
                for t in range(NT_P, NT):
                    pt, pb = proj_tm(slotB, slotB_b, 0, DV, t)
                    ACT.op(lambda t=t, pt=pt: nc.scalar.copy(out=vtm[:, t, :], in_=pt[:]), reads=[pb], writes=[v_b[t]])
            else:
                SP.dma(c_kv[0], [(kTo[h * DK:(h + 1) * DK, :].rearrange("(dc p) t -> p dc t", p=128), kT[:, :, 0:WP])],
                       reads=k_b[0:NT_P])
                SP.dma(c_kv[1], [(vo[:, h * DV:(h + 1) * DV].rearrange("(t p) v -> p t v", p=128), vtm[:, 0:NT_P, :])],
                       reads=v_b[0:NT_P])
            if isM:
                slotC, slotC_b = wload([(w_in[:, 4096 + h * DV:4096 + (h + 1) * DV], 8, 0)])
                DVE.op(lambda: nc.vector.memset(S[:], 0.0), writes=S_b)
                for c in range(NCORES - 1):
                    sl, slb, slc = sl_r[c % 2]
                    r0 = (c * HEADS + h) * 256
                    SP.dma(slc, [(sl[:], sloc[r0:r0 + 256, :].rearrange("(dc p) v -> p dc v", p=128))],
                           writes=[slb])
                    DVE.op(lambda sl=sl, c=c, h=h: nc.vector.scalar_tensor_tensor(
                        out=S[:], in0=sl[:], scalar=coef[:, h, c:c + 1], in1=S[:], op0=ALU.mult, op1=ALU.add),
                        reads=[slb, coef_b] + S_b, writes=S_b)
            else:
                DVE.op(lambda: nc.vector.memset(S[:], 0.0), writes=S_b)
            ACT.op(lambda: nc.scalar.copy(out=S_bf[:], in_=S[:]), reads=S_b, writes=Sbf_b)
            if h + 1 < HEADS and not isM:
                early[h + 1] = make_early(h + 1)
                nxt = early[h + 1][1]
            else:
                nxt = []

            dect = math.exp(128.0 * LG[h])
            s2_q = []
            s2a_q = []
            s3_q = []
            blk = {}

            def flush(upto_t, final=False):
                while s2_q and (final or s2_q[0][0] <= upto_t - 1):
                    s2_q.pop(0)[1]()
                while s3_q and (final or s3_q[0][0] <= upto_t - 2):
                    s3_q.pop(0)[1]()


            def ktrans(t):
                c0_ = t * 128
                pk, pkb = psT.next()
                pe_transposes([(pk[:, dc * 128:(dc + 1) * 128], kT[:, dc, c0_:c0_ + 128]) for dc in range(2)],
                              reads=[k_b[t]], writes=[pkb])
                kd, kdb = kdec_r.next()
                ACT.op(lambda: nc.scalar.activation(out=kd[:], in_=pk[:, 0:256], func=AF.Copy,
                                                    scale=dkcol[:, h:h + 1]),
                       reads=[pkb, dec_b], writes=[kdb])
                return kd, kdb

            def scores(t):
                c0_ = t * 128
                psc, pscb = psA[0]
                pe_group([(psc[:, 0:128], kT[:, dc, c0_:c0_ + 128], qT[:, dc, c0_:c0_ + 128], dc == 0, dc == 1)
                          for dc in range(2)], reads=[k_b[t], q_b[t]], writes=[pscb])
                sT, sTb = sT_r.next()
                DVE.op(lambda: nc.vector.tensor_tensor(out=sT[:], in0=psc[:, 0:128], in1=d2s[:, h, :],
                                                       op=ALU.mult), reads=[pscb, dec_b], writes=[sTb])
                qd, qdb = qd_r.next()
                DVE.op(lambda: nc.vector.tensor_tensor(
                    out=qd[:], in0=qT[:, :, c0_:c0_ + 128],
                    in1=dqbc[:, h, :].unsqueeze(1).to_broadcast([128, 2, 128]), op=ALU.mult),
                    reads=[q_b[t], dec_b], writes=[qdb])
                return sT, sTb, qd, qdb

            kd_next = ktrans(0)
            sc_next = scores(0) if isM else None
            for t in range(NT):
                tb_i = t // 4
                c0 = t * 128
                if s2a_q:
                    s2a_q.pop(0)()
                kd, kdb = kd_next
                if t + 1 < NT:
                    kd_next = ktrans(t + 1)
                if isM:
                    sT, sTb, qd, qdb = sc_next
                if isM:
                    pg, pgb = psA[1]
                    pe_group([(pg[:], uT(kc, c0, 128), slotC[:, kc, :], kc == 0, kc == 7) for kc in range(8)],
                             reads=[slotC_b, big2_b[t]], writes=[pgb])
                    po, pob = psA[2 + (t % 2)]
                    pe_group([(po[:], sT[:], vtm[:, t, :], True, False),
                              (po[:], qd[:, 0, :], S_bf[:, 0, :], False, False),
                              (po[:], qd[:, 1, :], S_bf[:, 1, :], False, True)],
                             reads=[sTb, v_b[t], qdb, Sbf_b[0], Sbf_b[1]], writes=[pob])
                if isM:
                    d0, d0b = psA[4]
                    d1, d1b = psA[5]
                else:
                    d0, d0b = bank()
                    d1, d1b = bank()
                pe_group([(d0[:], kd[:, 0:128], vtm[:, t, :], True, True),
                          (d1[:], kd[:, 128:256], vtm[:, t, :], True, True)],
                         reads=[kdb, v_b[t]], writes=[d0b, d1b])
                if isM:
                    sg, sgb = sg_r.next()
                    ACT.op(lambda: nc.scalar.activation(out=sg[:], in_=pg[:], func=AF.Silu),
                           reads=[pgb], writes=[sgb])
                for dc, (dd, ddb) in enumerate([(d0, d0b), (d1, d1b)]):
                    DVE.op(lambda: nc.vector.scalar_tensor_tensor(out=S[:, dc, :], in0=S[:, dc, :], scalar=dect,
                                                                  in1=dd[:], op0=ALU.mult, op1=ALU.add),
                           reads=[ddb, S_b[dc], Sbf_b[dc]], writes=[S_b[dc]])
                    if t < NT - 1:
                        ACT.op(lambda: nc.scalar.copy(out=S_bf[:, dc, :], in_=S[:, dc, :]),
                               reads=[S_b[dc]], writes=[Sbf_b[dc]])
                if isM and t + 1 < NT:
                    sc_next = scores(t + 1)
                for ci in range(t * len(nxt) // NT, (t + 1) * len(nxt) // NT):
                    nxt[ci]()
                flush(t)
                if not isM:
                    continue

                oc, ocb = oc_r.next()
                sgg, sggb = sgg_r.next()
                if t % 4 == 0:
                    vt, vtb = var_r.next()
                    blk[tb_i] = dict(vt=vt, vtb=vtb, tiles=[])
                bi_ = blk[tb_i]
                bi_["tiles"].append((t, oc, ocb, sgg, sggb))

                def stage2a(t=t, po=po, pob=pob, oc=oc, ocb=ocb, bi_=bi_):
                    i_ = t % 4
                    vt, vtb = bi_["vt"], bi_["vtb"]
                    stt, stb = st_r.next()
                    DVE.op(lambda: nc.vector.bn_stats(out=stt[:, 0:6], in_=po[:]), reads=[pob], writes=[stb])
                    DVE.op(lambda: nc.vector.bn_aggr(out=vt[:, 2 * i_:2 * i_ + 2], in_=stt[:, 0:6]),
                           reads=[stb], writes=[vtb])
                    ACT.op(lambda: nc.scalar.activation(out=oc[:], in_=po[:], func=AF.Identity, scale=-1.0,
                                                        bias=vt[:, 2 * i_:2 * i_ + 1]),
                           reads=[pob, vtb], writes=[ocb])
                s2a_q.append(stage2a)

                def stage2(sg=sg, sgb=sgb, sgg=sgg, sggb=sggb):
                    DVE.op(lambda: nc.vector.tensor_tensor(out=sgg[:], in0=sg[:], in1=gng_bf[:, h * DV:(h + 1) * DV],
                                                           op=ALU.mult), reads=[sgb, prow_b], writes=[sggb])
                s2_q.append((t, stage2))

                if t % 4 == 3 or t == NT - 1:
                    def stage3(tb_i=tb_i, bi_=bi_):
                        vt, vtb = bi_["vt"], bi_["vtb"]
                        nt_ = len(bi_["tiles"])
                        for i in range(nt_):
                            ACT.op(lambda i=i: nc.scalar.activation(out=vt[:, 8 + i:9 + i],
                                                                    in_=vt[:, 2 * i + 1:2 * i + 2], func=AF.Sqrt,
                                                                    bias=biasc[:, B_LN:B_LN + 1]),
                                   reads=[vtb, biasc_b], writes=[vtb])
                        DVE.op(lambda: nc.vector.reciprocal(out=vt[:, 8:8 + nt_], in_=vt[:, 8:8 + nt_]),
                               reads=[vtb], writes=[vtb])
                        DVE.op(lambda: nc.vector.tensor_scalar(out=vt[:, 8:8 + nt_], in0=vt[:, 8:8 + nt_],
                                                               scalar1=-1.0, scalar2=None, op0=ALU.mult),
                               reads=[vtb], writes=[vtb])
                        gst, gstb, gstc = goT_st[tb_i % 2]
                        for i, (t_, oc, ocb, sgg, sggb) in enumerate(bi_["tiles"]):
                            go, gob = go_r.next()
                            DVE.op(lambda: nc.vector.scalar_tensor_tensor(out=go[:], in0=oc[:],
                                                                          scalar=vt[:, 8 + i:9 + i], in1=sgg[:],
                                                                          op0=ALU.mult, op1=ALU.mult),
                                   reads=[ocb, sggb, vtb], writes=[gob])
                            pgt, pgtb = psT.next()
                            pe_transposes([(pgt[:, fc * 128:(fc + 1) * 128], go[:, fc * 128:(fc + 1) * 128])
                                           for fc in range(4)], reads=[gob], writes=[pgtb])
                            j0 = i * 128
                            ACT.op(lambda: nc.scalar.copy(out=gst[:, :, j0:j0 + 128],
                                                          in_=pgt[:, 0:512].rearrange("p (f t) -> p f t", f=4)),
                                   reads=[pgtb], writes=[gstb])
                        s_, n_ = TB[tb_i]
                        SP.dma(gstc, [(goT_d[h * DV:(h + 1) * DV, s_:s_ + n_].rearrange("(f p) t -> p f t", p=128),
                                       gst[:, :, 0:n_])], reads=[gstb], writes=[goT_db[h][tb_i]])
                    s3_q.append((t, stage3))
            while s2a_q:
                s2a_q.pop(0)()
            flush(NT, final=True)
            if not isM:
                SP.dma(c_so, [(sout[h * 256:(h + 1) * 256, :].rearrange("(dc p) v -> p dc v", p=128), S[:])],
                       reads=S_b)

        sB.__exit__(None, None, None)
        if not isM:
            k.finish()
            return nc

        hc_b = [Buf(f"hc{t}") for t in range(NT)]
        c_hst = [k.chan("c_hst0"), k.chan("c_hst1")]
        hres_r = Rot(k, "hres", [128, D], F32, 2)

        def load_hcur(t):
            tile_, b, ch = hload[hload_i[0] % len(hload)]
            hload_i[0] += 1
            SP.dma(ch, [(tile_[:], hcur[t * 128:(t + 1) * 128, :])], reads=[hc_b[t]], writes=[b])
            return tile_[:], b

        class HRing:
            def __init__(self, name, n):
                self.slots = [(k.sb(f"{name}{i}", [128, D], F32), Buf(f"{name}{i}"), k.chan(f"c_{name}{i}"))
                              for i in range(n)]
                self.i = 0
                self.loaded = {}

            def prefetch(self, t, src_ap, reads=()):
                tile_, b, ch = self.slots[self.i % len(self.slots)]
                self.i += 1
                SP.dma(ch, [(tile_[:], src_ap)], reads=list(reads), writes=[b])
                self.loaded[t] = (tile_, b)

            def update(self, t, psums):
                tile_, b = self.loaded.pop(t)
                for half, (pt, pb) in enumerate(psums):
                    DVE.op(lambda: nc.vector.tensor_tensor(out=tile_[:, half * 512:(half + 1) * 512], in0=pt[:],
                                                           in1=tile_[:, half * 512:(half + 1) * 512], op=ALU.add),
                           reads=[pb, b], writes=[b])
                SP.dma(c_hst[t % 2], [(hcur[t * 128:(t + 1) * 128, :], tile_[:])], reads=[b], writes=[hc_b[t]])

        HALO = 32
        sC = k.scope()
        sC.__enter__()
        ccT = k.sb("ccT", [128, 8, W], BF16)
        cc_b = [Buf(f"cc{i}") for i in range(len(TB))]
        with k.scope():
            ctc_r = Rot(k, "ctc", [128, HALO + W], BF16, 2)
            dg_r = Rot(k, "dg", [128, CW, 128], BF16, 2)
            sig_r = Rot(k, "sig", [128, 512], F32, 2)
            for cb2 in range(4):
                pieces = []
                for j in range(2):
                    cb = cb2 * 2 + j
                    pieces.append((w_in[:, 6144 + cb * 128:6144 + (cb + 1) * 128], 8, j * 256))
                    pieces.append((w_in[:, 7168 + cb * 128:7168 + (cb + 1) * 128], 8, j * 256 + 128))
                slot, slot_b = wload(pieces)
                for j in range(2):
                    cb = cb2 * 2 + j
                    ctc, ctcb = ctc_r.next()
                    DVE.op(lambda: nc.vector.memset(ctc[:, 0:HALO], 0.0), writes=[ctcb])
                    for (s, n) in TB:
                        pa, pab = proj_fm(slot, slot_b, j * 256, s, n)
                        pb2, pbb = proj_fm(slot, slot_b, j * 256 + 128, s, n)
                        sg, sgb = sig_r.next()
                        ACT.op(lambda: nc.scalar.activation(out=sg[:, 0:n], in_=pb2[:, 0:n], func=AF.Sigmoid),
                               reads=[pbb], writes=[sgb])
                        DVE.op(lambda: nc.vector.tensor_tensor(out=ctc[:, HALO + s:HALO + s + n], in0=pa[:, 0:n],
                                                               in1=sg[:, 0:n], op=ALU.mult),
                               reads=[pab, sgb], writes=[ctcb])
                    dg, dgb = dg_r.next()
                    DVE.op(lambda: nc.vector.tensor_tensor(
                        out=dg[:],
                        in0=ident[:].unsqueeze(1).to_broadcast([128, CW, 128]),
                        in1=pcol[:, O_CDW + cb * CW:O_CDW + (cb + 1) * CW].unsqueeze(2).to_broadcast([128, CW, 128]),
                        op=ALU.mult), reads=[ident_b, pcol_b], writes=[dgb])
                    for bi, (s, n) in enumerate(TB):
                        pc, pcb = bank()
                        o0 = HALO + s - (CW - 1)
                        pe_group([(pc[:, 0:n], dg[:, jj, :], ctc[:, o0 + jj:o0 + jj + n], jj == 0, jj == CW - 1)
                                  for jj in range(CW)], reads=[dgb, ctcb], writes=[pcb])
                        ACT.op(lambda: nc.scalar.activation(out=ccT[:, cb, s:s + n], in_=pc[:, 0:n],
                                                            func=AF.Identity,
                                                            bias=pcol[:, O_CDB + cb:O_CDB + cb + 1]),
                               reads=[pcb, pcol_b], writes=[cc_b[bi]])

        with k.scope():
            wcv0, wcv0_b = wload([(w_cvo[:, 0:512], 8, 0)])
            wcv1, wcv1_b = wload([(w_cvo[:, 512:1024], 8, 0)])
            wcv = [(wcv0, wcv0_b), (wcv1, wcv1_b)]
            wm0, wm0_b = wload([(w_mix[:, 0:512], 8, 0)])
            wm1, wm1_b = wload([(w_mix[:, 512:1024], 8, 0)])
            ones_bf = k.sb("ones_bf", [128, 128], BF16)
            ones_b = Buf("ones")
            DVE.op(lambda: nc.vector.memset(ones_bf[:], 1.0), writes=[ones_b])
            sq_r = Rot(k, "sq", [128, 8, 512], BF16, 1)
            lnT_r = Rot(k, "lnT", [128, 8, 512], BF16, 1)
            mean_r = Rot(k, "mean", [128, 512], F32, 1)
            rstd_r = Rot(k, "rstdt", [128, 512], F32, 1)
            gl, glb, glc = k.sb("goTl", [128, 16, 512], BF16), Buf("goTl"), k.chan("c_gl")
            ga_r = Rot(k, "ga", [128, 512], F32, 2)
            gb_r = Rot(k, "gb", [128, 512], F32, 2)
            yb_r = Rot(k, "yb", [128, 512], F32, 2)
            wretl_r = [(k.sb(f"wretl{i}", [128, 16, 128], BF16), Buf(f"wretl{i}"), k.chan(f"c_wr{i}"))
                       for i in range(2)]
            wgl_r = [(k.sb(f"wgl{i}", [128, 8, 256], BF16), Buf(f"wgl{i}"), k.chan(f"c_wg{i}")) for i in range(2)]
            wl_i = 0

            for bi, (s, n) in enumerate(TB):
                sq, sqb = sq_r.next()
                DVE.op(lambda: nc.vector.tensor_tensor(out=sq[:, :, 0:n], in0=ccT[:, :, s:s + n],
                                                       in1=ccT[:, :, s:s + n], op=ALU.mult),
                       reads=[cc_b[bi]], writes=[sqb])
                p1, p1b = bank()
                p2, p2b = bank()
                pe_group([(p1[:, 0:n], ones_bf[:], ccT[:, cb, s:s + n], cb == 0, cb == 7) for cb in range(8)],
                         reads=[ones_b, cc_b[bi]], writes=[p1b])
                pe_group([(p2[:, 0:n], ones_bf[:], sq[:, cb, 0:n], cb == 0, cb == 7) for cb in range(8)],
                         reads=[ones_b, sqb], writes=[p2b])
                mean, meanb = mean_r.next()
                rstd, rstdb = rstd_r.next()
                ACT.op(lambda: nc.scalar.activation(out=mean[:, 0:n], in_=p1[:, 0:n], func=AF.Copy, scale=1.0 / D),
                       reads=[p1b], writes=[meanb])
                DVE.op(lambda: nc.vector.tensor_tensor(out=rstd[:, 0:n], in0=mean[:, 0:n], in1=mean[:, 0:n],
                                                       op=ALU.mult), reads=[meanb], writes=[rstdb])
                DVE.op(lambda: nc.vector.scalar_tensor_tensor(out=rstd[:, 0:n], in0=p2[:, 0:n], scalar=1.0 / D,
                                                              in1=rstd[:, 0:n], op0=ALU.mult, op1=ALU.subtract),
                       reads=[p2b, rstdb], writes=[rstdb])
                ACT.op(lambda: nc.scalar.activation(out=rstd[:, 0:n], in_=rstd[:, 0:n], func=AF.Sqrt,
                                                    bias=biasc[:, B_LN:B_LN + 1]),
                       reads=[rstdb, biasc_b], writes=[rstdb])
                DVE.op(lambda: nc.vector.reciprocal(out=rstd[:, 0:n], in_=rstd[:, 0:n]),
                       reads=[rstdb], writes=[rstdb])
                lnT, lnTb = lnT_r.next()
                for cb in range(8):
                    t1, t1b = tmpf.next()
                    DVE.op(lambda: nc.vector.tensor_tensor(out=t1[:, 0:n], in0=ccT[:, cb, s:s + n],
                                                           in1=mean[:, 0:n], op=ALU.subtract),
                           reads=[cc_b[bi], meanb], writes=[t1b])
                    DVE.op(lambda: nc.vector.tensor_tensor(out=t1[:, 0:n], in0=t1[:, 0:n], in1=rstd[:, 0:n],
                                                           op=ALU.mult), reads=[t1b, rstdb], writes=[t1b])
                    ACT.op(lambda: nc.scalar.activation(out=lnT[:, cb, 0:n], in_=t1[:, 0:n], func=AF.Silu,
                                                        scale=pcol[:, O_CLG + cb:O_CLG + cb + 1],
                                                        bias=pcol[:, O_CLB + cb:O_CLB + cb + 1]),
                           reads=[t1b, pcol_b], writes=[lnTb])
                SP.dma(glc, [(gl[:, :, 0:n], goT_d[:, s:s + n].rearrange("(f p) t -> p f t", p=128))],
                       reads=[goT_db[h][bi] for h in range(HEADS)], writes=[glb])
                for db in range(8):
                    wretl, wretl_b, c_wr = wretl_r[wl_i % 2]
                    wgl, wgl_b, c_wg = wgl_r[wl_i % 2]
                    wl_i += 1
                    POOL.dma(c_wg, [(wgl[:, :, 0:128],
                                     w_in[:, 8192 + db * 128:8192 + (db + 1) * 128]
                                     .rearrange("(kc p) n -> p kc n", p=128)),
                                    (wgl[:, :, 128:256],
                                     w_in[:, 9216 + db * 128:9216 + (db + 1) * 128]
                                     .rearrange("(kc p) n -> p kc n", p=128))],
                             writes=[wgl_b])
                    POOL.dma(c_wr, [(wretl[:], w_ret[:, db * 128:(db + 1) * 128]
                                     .rearrange("(kc p) n -> p kc n", p=128))], writes=[wretl_b])
                    wt, wtb = wcv[db // 4]
                    co = (db % 4) * 128
                    pyb, pybb = bank()
                    pe_group([(pyb[:, 0:n], wt[:, cb, co:co + 128], lnT[:, cb, 0:n], cb == 0, cb == 7)
                              for cb in range(8)], reads=[wtb, lnTb], writes=[pybb])
                    yb, ybb = yb_r.next()
                    ACT.op(lambda: nc.scalar.activation(out=yb[:, 0:n], in_=pyb[:, 0:n], func=AF.Identity,
                                                        bias=pcol[:, O_BCO + db:O_BCO + db + 1]),
                           reads=[pybb, pcol_b], writes=[ybb])
                    pga, pgab = proj_fm(wgl, wgl_b, 0, s, n)
                    pgb_, pgbb = proj_fm(wgl, wgl_b, 128, s, n)
                    ga, gab = ga_r.next()
                    gb, gbb = gb_r.next()
                    ACT.op(lambda: nc.scalar.activation(out=ga[:, 0:n], in_=pga[:, 0:n], func=AF.Sigmoid,
                                                        bias=pcol[:, O_BGATE + db:O_BGATE + db + 1]),
                           reads=[pgab, pcol_b], writes=[gab])
                    ACT.op(lambda: nc.scalar.activation(out=gb[:, 0:n], in_=pgb_[:, 0:n], func=AF.Sigmoid,
                                                        bias=pcol[:, O_BGATE + 8 + db:O_BGATE + 8 + db + 1]),
                           reads=[pgbb, pcol_b], writes=[gbb])
                    pya, pyab = bank()
                    pe_group([(pya[:, 0:n], wretl[:, fc, :], gl[:, fc, 0:n], fc == 0, fc == 15)
                              for fc in range(16)], reads=[wretl_b, glb], writes=[pyab])
                    DVE.op(lambda: nc.vector.tensor_tensor(out=ga[:, 0:n], in0=pya[:, 0:n], in1=ga[:, 0:n],
                                                           op=ALU.mult), reads=[pyab, gab], writes=[gab])
                    DVE.op(lambda: nc.vector.tensor_tensor(out=gb[:, 0:n], in0=yb[:, 0:n], in1=gb[:, 0:n],
                                                           op=ALU.mult), reads=[ybb, gbb], writes=[gbb])
                    DVE.op(lambda: nc.vector.tensor_tensor(out=ccT[:, db, s:s + n], in0=ga[:, 0:n], in1=gb[:, 0:n],
                                                           op=ALU.add), reads=[gab, gbb], writes=[cc_b[bi]])

        ringC = HRing("hrc", 8)
        groups = [list(range(i, min(i + 4, NT))) for i in range(0, NT, 4)]
        for t in groups[0]:
            ringC.prefetch(t, hwin[t * 128:(t + 1) * 128, :])
        for gi, grp in enumerate(groups):
            if gi + 1 < len(groups):
                for t in groups[gi + 1]:
                    ringC.prefetch(t, hwin[t * 128:(t + 1) * 128, :])
            for t in grp:
                psums = []
                for (wt, wtb) in [(wm0, wm0_b), (wm1, wm1_b)]:
                    pt, pb = bank()
                    pe_group([(pt[:], ccT[:, kc, t * 128:(t + 1) * 128], wt[:, kc, :], kc == 0, kc == 7)
                              for kc in range(8)], reads=[cc_b[t // 4], wtb], writes=[pb])
                    psums.append((pt, pb))
                ringC.update(t, psums)
        sC.__exit__(None, None, None)
        if dbg:
            c_dbg = k.chan("c_dbg")
            for t in range(NT):
                SP.dma(c_dbg, [(dbg_h1[t * 128:(t + 1) * 128, :], hcur[t * 128:(t + 1) * 128, :])], reads=[hc_b[t]])

        with k.scope():
            memT = k.sb("memT", [128, 8, MEM], BF16)
            memT_b = [Buf("memT0"), Buf("memT1")]
            meml = [(k.sb(f"meml{i}", [128, D], F32), Buf(f"meml{i}")) for i in range(2)]
            c_mem = k.chan("c_mem")
            for i in range(2):
                SP.dma(c_mem, [(meml[i][0][:], mem_d[i * 128:(i + 1) * 128, :])], writes=[meml[i][1]])
            rms_norm_T(lambda t: (meml[t][0][:], meml[t][1]), 2, O_NMEM, memT, memT_b, dst_col0=0)
            kmT = k.sb("kmT", [128, 8, MEM], BF16)
            vm = k.sb("vm", [128, 2, D], BF16)
            km_b = Buf("kmT")
            vm_b = Buf("vm")
            for half in range(2):
                wk, wkb = wload([(w_xkv[:, half * 512:(half + 1) * 512], 8, 0)])
                for g in range(4):
                    pt, pb = bank()
                    pe_group([(pt[:, 0:MEM], wk[:, kc, g * 128:(g + 1) * 128], memT[:, kc, :], kc == 0, kc == 7)
                              for kc in range(8)], reads=[wkb] + memT_b, writes=[pb])
                    ACT.op(lambda: nc.scalar.copy(out=kmT[:, half * 4 + g, :], in_=pt[:, 0:MEM]),
                           reads=[pb], writes=[km_b])
            for half in range(2):
                wv, wvb = wload([(w_xkv[:, 1024 + half * 512:1024 + (half + 1) * 512], 8, 0)])
                for mc in range(2):
                    pt, pb = bank()
                    pe_group([(pt[:], memT[:, kc, mc * 128:(mc + 1) * 128], wv[:, kc, :], kc == 0, kc == 7)
                              for kc in range(8)], reads=[wvb] + memT_b, writes=[pb])
                    ACT.op(lambda: nc.scalar.copy(out=vm[:, mc, half * 512:(half + 1) * 512], in_=pt[:]),
                           reads=[pb], writes=[vm_b])

            rms_norm_T(load_hcur, NT, O_NXA, big2, big2_b)
            wq0, wq0_b = wload([(w_xq[:, 0:512], 8, 0)])
            wq1, wq1_b = wload([(w_xq[:, 512:1024], 8, 0)])
            wo0, wo0_b = wload([(w_xo[:, 0:512], 8, 0)])
            wo1, wo1_b = wload([(w_xo[:, 512:1024], 8, 0)])
            qx_r = Rot(k, "qx", [128, 8, 512], BF16, 1)
            ox_r = Rot(k, "ox", [128, 8, 512], BF16, 1)
            pexp_r = Rot(k, "pexp", [128, 4, MEM], F32, 2)
            pn_r = Rot(k, "pn", [128, 4, MEM], BF16, 2)
            pT_r = Rot(k, "pT", [128, 8, 128], BF16, 2)
            ringD = HRing("hrd", 8)
            for bi, (s, n) in enumerate(TB):
                for t in tiles_of(s, n):
                    ringD.prefetch(t, hcur[t * 128:(t + 1) * 128, :], reads=[hc_b[t]])
                qx, qxb = qx_r.next()
                for g in range(8):
                    wt, wtb = (wq0, wq0_b) if g < 4 else (wq1, wq1_b)
                    pt, pb = proj_fm(wt, wtb, (g % 4) * 128, s, n)
                    ACT.op(lambda: nc.scalar.copy(out=qx[:, g, 0:n], in_=pt[:, 0:n]), reads=[pb], writes=[qxb])
                ox, oxb = ox_r.next()

                def part_a(t):
                    j0 = t * 128 - s
                    sc = [bank(), bank()]
                    for hp in range(2):
                        mms = []
                        for hh in range(2):
                            hd = hp * 2 + hh
                            for dc in range(2):
                                mms.append((sc[hp][0][:, hh * MEM:(hh + 1) * MEM], qx[:, hd * 2 + dc, j0:j0 + 128],
                                            kmT[:, hd * 2 + dc, :], dc == 0, dc == 1))
                        pe_group(mms, reads=[qxb, km_b], writes=[sc[hp][1]])
                    ct, cb = col4.next()
                    for hp in range(2):
                        DVE.op(lambda hp=hp: nc.vector.tensor_reduce(
                            out=ct[:, hp * 2:hp * 2 + 2], in_=sc[hp][0][:].rearrange("p (h m) -> p h m", h=2),
                            axis=mybir.AxisListType.X, op=ALU.max), reads=[sc[hp][1]], writes=[cb])
                    DVE.op(lambda: nc.vector.tensor_scalar(out=ct[:, 0:4], in0=ct[:, 0:4], scalar1=-1.0 / 16.0,
                                                           scalar2=None, op0=ALU.mult), reads=[cb], writes=[cb])
                    pe_, peb = pexp_r.next()
                    for hd in range(4):
                        ACT.op(lambda hd=hd: nc.scalar.activation(
                            out=pe_[:, hd, :], in_=sc[hd // 2][0][:, (hd % 2) * MEM:(hd % 2 + 1) * MEM],
                            func=AF.Exp, scale=1.0 / 16.0, bias=ct[:, hd:hd + 1], accum_out=ct[:, 4 + hd:5 + hd]),
                            reads=[sc[hd // 2][1], cb], writes=[peb, cb])
                    DVE.op(lambda: nc.vector.reciprocal(out=ct[:, 4:8], in_=ct[:, 4:8]), reads=[cb], writes=[cb])
                    pn, pnb = pn_r.next()
                    DVE.op(lambda: nc.vector.tensor_tensor(out=pn[:], in0=pe_[:],
                                                           in1=ct[:, 4:8].unsqueeze(2).to_broadcast([128, 4, MEM]),
                                                           op=ALU.mult), reads=[peb, cb], writes=[pnb])
                    return pn, pnb

                def part_b(t, pn, pnb):
                    j0 = t * 128 - s
                    ptt, pttb = psT.next()
                    pe_transposes([(ptt[:, (hd * 2 + mc) * 128:(hd * 2 + mc + 1) * 128],
                                    pn[:, hd, mc * 128:(mc + 1) * 128]) for hd in range(4) for mc in range(2)],
                                  reads=[pnb], writes=[pttb])
                    pT, pTb = pT_r.next()
                    ACT.op(lambda: nc.scalar.copy(out=pT[:], in_=ptt[:].rearrange("p (g t) -> p g t", g=8)),
                           reads=[pttb], writes=[pTb])
                    po2 = [bank(), bank()]
                    for hp in range(2):
                        mms = []
                        for gg in range(4):
                            g = hp * 4 + gg
                            hd = g // 2
                            for mc in range(2):
                                mms.append((po2[hp][0][:, gg * 128:(gg + 1) * 128],
                                            vm[:, mc, g * 128:(g + 1) * 128],
                                            pT[:, hd * 2 + mc, :], mc == 0, mc == 1))
                        pe_group(mms, reads=[vm_b, pTb], writes=[po2[hp][1]])
                        ACT.op(lambda hp=hp: nc.scalar.copy(
                            out=ox[:, hp * 4:(hp + 1) * 4, j0:j0 + 128],
                            in_=po2[hp][0][:].rearrange("p (g t) -> p g t", g=4)),
                            reads=[po2[hp][1]], writes=[oxb])

                tl = tiles_of(s, n)
                prev = None
                for t in tl:
                    cur = (t,) + part_a(t)
                    if prev is not None:
                        part_b(*prev)
                    prev = cur
                part_b(*prev)
                for t in tiles_of(s, n):
                    j0 = t * 128 - s
                    psums = []
                    for (wt, wtb) in [(wo0, wo0_b), (wo1, wo1_b)]:
                        pt, pb = bank()
                        pe_group([(pt[:], ox[:, kc, j0:j0 + 128], wt[:, kc, :], kc == 0, kc == 7)
                                  for kc in range(8)], reads=[oxb, wtb], writes=[pb])
                        psums.append((pt, pb))
                    ringD.update(t, psums)

        if dbg:
            for t in range(NT):
                SP.dma(c_dbg, [(dbg_h2[t * 128:(t + 1) * 128, :], hcur[t * 128:(t + 1) * 128, :])], reads=[hc_b[t]])
        with k.scope():
            h_sb = k.sb("h_sb", [128, NT, D], F32)
            h_b = [Buf(f"h{t}") for t in range(NT)]
            c_hl = [k.chan("c_hl0"), k.chan("c_hl1")]
            for t in range(NT):
                SP.dma(c_hl[t % 2], [(h_sb[:, t, :], hcur[t * 128:(t + 1) * 128, :])],
                       reads=[hc_b[t]], writes=[h_b[t]])
            rms_norm_T(lambda t: (h_sb[:, t, :], h_b[t]), NT, O_NFFN, big2, big2_b)
            tmask = k.sb("tmask", [128, 128], F32)
            tmask_b = Buf("tmask")
            SP.dma(c_misc, [(tmask[:], tmask_d[:, :])], writes=[tmask_b])
            DVE.op(lambda: nc.vector.tensor_tensor(out=big2[:, :, 2:130], in0=big2[:, :, 2:130],
                                                   in1=tmask[:].unsqueeze(1).to_broadcast([128, 8, 128]),
                                                   op=ALU.mult), reads=[big2_b[0], tmask_b], writes=[big2_b[0]])
            FB = []
            s = 0
            while s < W:
                n = min(510, W - s)
                FB.append((s, n))
                s += n
            actT = k.sb("actT", [128, 4, W], BF16)
            act_b = Buf("actT")
            acc_r = tmpf
            sl_r2 = tmpf
            wd = k.sb("wd", [128, 4, D], BF16)
            wd_b = Buf("wd")
            c_wd = k.chan("c_wd")
            for g in range((NFB + 3) // 4):
                nfb = min(4, NFB - g * 4)
                wval, wval_b = wload([(w_up[:, g * 512:g * 512 + nfb * 128], 8, 0)])
                wgat, wgat_b = wload([(w_up[:, FFN + g * 512:FFN + g * 512 + nfb * 128], 8, 0)])
                POOL.dma(c_wd, [(wd[:, 0:nfb, :],
                                 w_down[g * 512:g * 512 + nfb * 128, :].rearrange("(kc p) n -> p kc n", p=128))],
                         writes=[wd_b])
                for fi in range(nfb):
                    fb = g * 4 + fi
                    for (s, n) in FB:
                        pgt, pgtb = bank()
                        pe_group([(pgt[:, 0:n + 2], wgat[:, kc, fi * 128:(fi + 1) * 128], big2[:, kc, s:s + n + 2],
                                   kc == 0, kc == 7) for kc in range(8)],
                                 reads=[wgat_b, big2_halo] + [big2_b[t] for t in tiles_of(max(s - 2, 0), n + 2)
                                                              if t < NT],
                                 writes=[pgtb])
                        pv, pvb = proj_fm(wval, wval_b, fi * 128, s, n)
                        acc, accb = acc_r.next()
                        w0 = pcol[:, O_FDW + fb * 3 + 0:O_FDW + fb * 3 + 1]
                        w1 = pcol[:, O_FDW + fb * 3 + 1:O_FDW + fb * 3 + 2]
                        w2 = pcol[:, O_FDW + fb * 3 + 2:O_FDW + fb * 3 + 3]
                        bb = pcol[:, O_FDB + fb:O_FDB + fb + 1]
                        DVE.op(lambda: nc.vector.tensor_scalar(out=acc[:, 0:n], in0=pgt[:, 2:n + 2], scalar1=w2,
                                                               scalar2=bb, op0=ALU.mult, op1=ALU.add),
                               reads=[pgtb, pcol_b], writes=[accb])
                        DVE.op(lambda: nc.vector.scalar_tensor_tensor(out=acc[:, 0:n], in0=pgt[:, 1:n + 1],
                                                                      scalar=w1, in1=acc[:, 0:n],
                                                                      op0=ALU.mult, op1=ALU.add),
                               reads=[pgtb, pcol_b, accb], writes=[accb])
                        DVE.op(lambda: nc.vector.scalar_tensor_tensor(out=acc[:, 0:n], in0=pgt[:, 0:n],
                                                                      scalar=w0, in1=acc[:, 0:n],
                                                                      op0=ALU.mult, op1=ALU.add),
                               reads=[pgtb, pcol_b, accb], writes=[accb])
                        sl, slb = sl_r2.next()
                        ACT.op(lambda: nc.scalar.activation(out=sl[:, 0:n], in_=acc[:, 0:n], func=AF.Silu),
                               reads=[accb], writes=[slb])
                        DVE.op(lambda: nc.vector.tensor_tensor(out=actT[:, fi, s:s + n], in0=pv[:, 0:n],
                                                               in1=sl[:, 0:n], op=ALU.mult),
                               reads=[pvb, slb], writes=[act_b])
                for t in range(NT):
                    for half in range(2):
                        pt, pb = bank()
                        pe_group([(pt[:], actT[:, fi, t * 128:(t + 1) * 128], wd[:, fi, half * 512:(half + 1) * 512],
                                   fi == 0, fi == nfb - 1) for fi in range(nfb)],
                                 reads=[act_b, wd_b], writes=[pb])
                        DVE.op(lambda: nc.vector.tensor_tensor(out=h_sb[:, t, half * 512:(half + 1) * 512],
                                                               in0=pt[:],
                                                               in1=h_sb[:, t, half * 512:(half + 1) * 512],
                                                               op=ALU.add),
                               reads=[pb, h_b[t]], writes=[h_b[t]])

            gfin = k.sb("gfin", [128, D], F32)
            gfin_b = Buf("gfin")
            SP.dma(c_misc, [(gfin[:], prow_d[:, 2048:3072])], writes=[gfin_b])
            c_out = [k.chan("c_out0"), k.chan("c_out1")]
            no_r = hres_r
            for t in range(1, NT):
                r0 = (t - 1) * 128
                SP.dma(c_out[0], [(hout[r0:r0 + 128, :], h_sb[:, t, :])], reads=[h_b[t]])
                jt, jb = junk.next()
                ct, cb = col4.next()
                ACT.op(lambda: nc.scalar.activation(out=jt[:], in_=h_sb[:, t, :], func=AF.Square,
                                                    accum_out=ct[:, 0:1]), reads=[h_b[t]], writes=[jb, cb])
                ACT.op(lambda: nc.scalar.activation(out=ct[:, 1:2], in_=ct[:, 0:1], func=AF.Sqrt, scale=1.0 / D,
                                                    bias=biasc[:, B_RMS:B_RMS + 1]),
                       reads=[cb, biasc_b], writes=[cb])
                DVE.op(lambda: nc.vector.reciprocal(out=ct[:, 2:3], in_=ct[:, 1:2]), reads=[cb], writes=[cb])
                no, nob = no_r.next()
                DVE.op(lambda: nc.vector.scalar_tensor_tensor(out=no[:], in0=h_sb[:, t, :], scalar=ct[:, 2:3],
                                                              in1=gfin[:], op0=ALU.mult, op1=ALU.mult),
                       reads=[h_b[t], cb, gfin_b], writes=[nob])
                SP.dma(c_out[1], [(nout[r0:r0 + 128, :], no[:])], reads=[nob])
        k.finish()
    return nc


_PROGS = {}


def _prog(mode):
    if mode not in _PROGS:
        _PROGS[mode] = build(mode)
    return _PROGS[mode]


def _colpack(v):
    v = np.asarray(v, np.float32)
    return np.ascontiguousarray(v.reshape(-1, 128).T)


def _pcol(inp, l):
    cols = [
        _colpack(inp["norm_mix_g"][l]),
        _colpack(inp["b_gate"][l]),
        np.ascontiguousarray(np.asarray(inp["conv_dw_w"][l], np.float32).reshape(CW, 8, 128).transpose(2, 1, 0)
                             .reshape(128, 8 * CW)),
        _colpack(inp["conv_dw_b"][l]),
        _colpack(inp["conv_ln_g"][l]),
        _colpack(inp["conv_ln_b"][l]),
        _colpack(inp["b_conv_out"][l]),
        _colpack(inp["norm_xattn_g"][l]),
        _colpack(inp["norm_mem_g"][l]),
        _colpack(inp["norm_ffn_g"][l]),
        np.ascontiguousarray(np.asarray(inp["ffn_dw_w"][l], np.float32).reshape(3, NFB, 128).transpose(2, 1, 0)
                             .reshape(128, NFB * 3)),
        _colpack(inp["ffn_dw_b"][l]),
    ]
    out = np.ascontiguousarray(np.concatenate(cols, axis=1))
    assert out.shape == (128, NPC)
    return out


def kernel(**inp):
    x = np.asarray(inp["x"], np.float32)[0]
    mem = np.ascontiguousarray(np.asarray(inp["mem"], np.float32)[0])
    pos = np.asarray(inp["positions"], np.int32)[0]
    cores = list(range(NCORES))
    WM = NT_M * 128

    def windows(arr, fill=0):
        pad = np.full((EXT,) + arr.shape[1:], fill, arr.dtype)
        ext = np.concatenate([pad, arr], axis=0)
        return [np.ascontiguousarray(ext[c * TOK:c * TOK + WM]) for c in cores]

    posw = [np.ascontiguousarray(np.broadcast_to(w[None, :], (128, WM))) for w in windows(pos)]
    expo = []
    for c in cores:
        e = np.array([float(c - 1 - cp) if cp < c else 1.0e4 for cp in cores], np.float32)
        expo.append(np.ascontiguousarray(np.broadcast_to(e[None, :], (128, NCORES))))

    tmask = [np.full((128, 128), 0.0 if c == 0 else 1.0, np.float32) for c in cores]
    h = x
    nrm = None
    for l in range(DEPTH):
        hw = windows(h)
        pcol = _pcol(inp, l)
        w_in = np.ascontiguousarray(np.asarray(inp["w_in"][l], np.float32))
        resP = run_bass_kernel_spmd(
            _prog("P"),
            [{"hwin": hw[c], "posw": posw[c], "w_in": w_in, "pcol": pcol} for c in cores],
            core_ids=cores)
        sloc = np.ascontiguousarray(np.concatenate([resP.results[c]["sout"] for c in cores], axis=0))
        kTi = [resP.results[c]["kTo"] for c in cores]
        uTi = [resP.results[c]["uTo"] for c in cores]
        csi = [resP.results[c]["cso"] for c in cores]
        vi = [resP.results[c]["vo"] for c in cores]
        prow = np.ascontiguousarray(np.broadcast_to(
            np.concatenate([np.asarray(inp["ret_gn_g"][l], np.float32),
                            np.asarray(inp["norm_final_g"], np.float32)])[None, :], (128, 3072)))
        common = {
            "w_in": w_in, "pcol": pcol, "sloc": sloc, "mem": mem, "prow": prow,
            "w_ret_out": np.ascontiguousarray(np.asarray(inp["w_ret_out"][l], np.float32)),
            "w_conv_out": np.ascontiguousarray(np.asarray(inp["w_conv_out"][l], np.float32)),
            "w_mix_out": np.ascontiguousarray(np.asarray(inp["w_mix_out"][l], np.float32)),
            "w_xq": np.ascontiguousarray(np.asarray(inp["w_xq"][l], np.float32)),
            "w_xkv": np.ascontiguousarray(np.asarray(inp["w_xkv"][l], np.float32)),
            "w_xo": np.ascontiguousarray(np.asarray(inp["w_xo"][l], np.float32)),
            "w_up": np.ascontiguousarray(np.asarray(inp["w_up"][l], np.float32)),
            "w_down": np.ascontiguousarray(np.asarray(inp["w_down"][l], np.float32)),
        }
        resM = run_bass_kernel_spmd(
            _prog("M"),
            [dict(common, hwin=hw[c], posw=posw[c], expo=expo[c], tmask=tmask[c], kTi=kTi[c], vi=vi[c], uTi=uTi[c], csi=csi[c]) for c in cores],
            core_ids=cores)
        h = np.concatenate([resM.results[c]["hout"] for c in cores], axis=0)
        nrm = [resM.results[c]["nout"] for c in cores]
    out = np.concatenate(nrm, axis=0).astype(np.float32)
    return out[None]
```
